# Optimizing a Trainium2 kernel written in Bass

```python
import jax, jax.numpy as jnp
from jax import lax
import numpy as np

D_MODEL = 4096
BATCH = 1
SEQ = 8192
DEPTH = 1

CONV_WIDTH = D_MODEL // 2
CONV_K = 3
POOL_WINDOWS = (2, 4, 8, 16)
N_POOL_GROUPS = len(POOL_WINDOWS)
POOL_WIDTH = D_MODEL // 2
POOL_GROUP = POOL_WIDTH // N_POOL_GROUPS
N_BRANCHES = 2
FFN_HIDDEN = ((8 * D_MODEL // 3 + 255) // 256) * 256
N_MOD = 6
IN_COLS = 3 * CONV_WIDTH + POOL_WIDTH + N_BRANCHES * D_MODEL
EPS = 1e-6

kernel_name = "hybrid_conv_pool_gated_block"


def rms_norm(x, gain):
    x32 = x.astype(jnp.float32)
    y = x32 * lax.rsqrt(jnp.mean(x32 * x32, axis=-1, keepdims=True) + EPS)
    return (y * gain.astype(jnp.float32)).astype(x.dtype)


def modulate(h, shift, scale):
    return h * (1.0 + scale[:, None, :]) + shift[:, None, :]


def causal_short_conv(u, w):
    s = u.shape[1]
    k_width = w.shape[0]
    up = jnp.pad(u, ((0, 0), (k_width - 1, 0), (0, 0)))
    y = w[0] * up[:, 0:s]
    for k in range(1, k_width):
        y = y + w[k] * up[:, k:k + s]
    return y


def causal_mean_minus_self(u, window):
    b, s, ch = u.shape
    u32 = u.astype(jnp.float32)
    cs = jnp.concatenate([jnp.zeros((b, 1, ch), jnp.float32), jnp.cumsum(u32, axis=1)], axis=1)
    hi = cs[:, 1:]
    lo = jnp.pad(cs[:, :s + 1 - window], ((0, 0), (window - 1, 0), (0, 0)))
    count = jnp.minimum(jnp.arange(1, s + 1), window).astype(jnp.float32)[None, :, None]
    return ((hi - lo) / count - u32).astype(u.dtype)


def hybrid_mixer(h, w_in, b_branch_gate, conv_w, w_conv_out, w_pool_group, pool_scale, w_pool_out, w_o):
    b, s, _ = h.shape
    proj = jnp.einsum('bsd,dn->bsn', h, w_in)
    c0 = CONV_WIDTH
    splits = [c0, 2 * c0, 3 * c0, 3 * c0 + POOL_WIDTH, 3 * c0 + POOL_WIDTH + D_MODEL]
    gate_b, gate_c, v, p, g_conv, g_pool = jnp.split(proj, splits, axis=-1)

    y_conv = jnp.einsum('bsc,cd->bsd', gate_b * causal_short_conv(gate_c * v, conv_w), w_conv_out)

    pg = p.reshape(b, s, N_POOL_GROUPS, POOL_GROUP)
    pooled = jnp.stack([causal_mean_minus_self(pg[:, :, g], POOL_WINDOWS[g]) for g in range(N_POOL_GROUPS)], axis=2)
    mixed = jnp.einsum('bsgc,gce->bsge', pooled, w_pool_group).reshape(b, s, POOL_WIDTH) * pool_scale
    y_pool = jnp.einsum('bsc,cd->bsd', mixed, w_pool_out)

    gb_conv, gb_pool = jnp.split(b_branch_gate, 2)
    merged = jax.nn.sigmoid(g_conv + gb_conv) * y_conv + jax.nn.sigmoid(g_pool + gb_pool) * y_pool
    return jnp.einsum('bsd,de->bse', merged, w_o)


def swiglu_ffn(h, w_gate_up, w_down):
    gu = jnp.einsum('bsd,df->bsf', h, w_gate_up)
    g, u = jnp.split(gu, 2, axis=-1)
    return jnp.einsum('bsf,fd->bsd', jax.nn.silu(g) * u, w_down)


def setup_inputs(seed: int = 0) -> dict:
    key = jax.random.key(seed)
    ks = jax.random.split(key, 20)
    f32 = jnp.float32
    nrm = lambda k, shape, scale: jax.random.normal(k, shape, f32) * scale
    return {
        "x": nrm(ks[0], (BATCH, SEQ, D_MODEL), 1.0),
        "c": nrm(ks[1], (BATCH, D_MODEL), 1.0),
        "w_ada": nrm(ks[2], (DEPTH, D_MODEL, N_MOD * D_MODEL), D_MODEL ** -0.5),
        "b_ada": nrm(ks[3], (DEPTH, N_MOD * D_MODEL), 0.02),
        "norm1_gain": 1.0 + nrm(ks[4], (DEPTH, D_MODEL), 0.05),
        "w_in": nrm(ks[5], (DEPTH, D_MODEL, IN_COLS), D_MODEL ** -0.5),
        "b_branch_gate": nrm(ks[6], (DEPTH, N_BRANCHES * D_MODEL), 0.02),
        "conv_w": nrm(ks[7], (DEPTH, CONV_K, CONV_WIDTH), CONV_K ** -0.5),
        "w_conv_out": nrm(ks[8], (DEPTH, CONV_WIDTH, D_MODEL), CONV_WIDTH ** -0.5),
        "w_pool_group": nrm(ks[9], (DEPTH, N_POOL_GROUPS, POOL_GROUP, POOL_GROUP), POOL_GROUP ** -0.5),
        "pool_scale": 1.0 + nrm(ks[10], (DEPTH, POOL_WIDTH), 0.05),
        "w_pool_out": nrm(ks[11], (DEPTH, POOL_WIDTH, D_MODEL), POOL_WIDTH ** -0.5),
        "w_o": nrm(ks[12], (DEPTH, D_MODEL, D_MODEL), D_MODEL ** -0.5),
        "norm2_gain": 1.0 + nrm(ks[13], (DEPTH, D_MODEL), 0.05),
        "w_gate_up": nrm(ks[14], (DEPTH, D_MODEL, 2 * FFN_HIDDEN), D_MODEL ** -0.5),
        "w_down": nrm(ks[15], (DEPTH, FFN_HIDDEN, D_MODEL), FFN_HIDDEN ** -0.5),
        "final_norm_gain": 1.0 + nrm(ks[16], (D_MODEL,), 0.05),
    }


def reference(x, c, w_ada, b_ada, norm1_gain, w_in, b_branch_gate, conv_w, w_conv_out, w_pool_group,
              pool_scale, w_pool_out, w_o, norm2_gain, w_gate_up, w_down, final_norm_gain):
    cs = jax.nn.silu(c)
    for l in range(DEPTH):
        mod = jnp.einsum('bd,dm->bm', cs, w_ada[l]) + b_ada[l]
        shift1, scale1, gate1, shift2, scale2, gate2 = jnp.split(mod, N_MOD, axis=-1)

        h = modulate(rms_norm(x, norm1_gain[l]), shift1, scale1)
        y = hybrid_mixer(h, w_in[l], b_branch_gate[l], conv_w[l], w_conv_out[l], w_pool_group[l],
                         pool_scale[l], w_pool_out[l], w_o[l])
        x = x + gate1[:, None, :] * y

        h = modulate(rms_norm(x, norm2_gain[l]), shift2, scale2)
        x = x + gate2[:, None, :] * swiglu_ffn(h, w_gate_up[l], w_down[l])
    return rms_norm(x, final_norm_gain)
```

```python
import contextlib
import numpy as np
import concourse.bass as bass
import concourse.mybir as mybir
from concourse.bass_utils import run_bass_kernel_spmd

F32 = mybir.dt.float32
BF16 = mybir.dt.bfloat16
U8 = mybir.dt.uint8
AF = mybir.ActivationFunctionType
ALU = mybir.AluOpType

POOL_WINDOWS = (2, 4, 8, 16)
EPS = 1e-6
GRAN = 256
SAME_ENG_SYNC = True


class Cfg:
    def __init__(self, D=4096, F=11008, S=8192, ncores=8, KP=16, NS=4, SLOT=16384):
        self.D, self.F, self.S, self.NCORES, self.KP, self.NS = D, F, S, ncores, KP, NS
        self.KC = D // 128
        self.CW = D // 2
        self.CC = self.CW // 128
        self.PGS = self.CW // 4
        self.CPG = self.PGS // 128
        self.FC = F // 128
        self.TPC = S // ncores
        self.TT = 512
        self.NT = self.TPC // self.TT
        self.HALO = 16
        self.TW = self.TT + self.HALO
        self.XW = min(D, 2048)
        self.DW = min(D, 2048)
        self.NV = 14 * self.KC
        self.NVB = (self.NV + 127) // 128
        self.SLOT = SLOT
        self.MC = 6 * self.KC
        assert self.KC % KP == 0 and (self.CC % KP == 0 or self.CC <= KP)


class Acc:
    __slots__ = ("ap", "res")

    def __init__(self, ap, res):
        self.ap = ap
        self.res = res


class Op:
    __slots__ = ("eng", "fn", "deps", "ev", "signal", "dma", "seq")

    def __init__(self, eng, fn, dma, seq):
        self.eng, self.fn, self.dma, self.seq = eng, fn, dma, seq
        self.deps = {}
        self.ev = None
        self.signal = False


ENGS = ("pe", "act", "dve", "pool", "sp")


class Sched:
    def __init__(self):
        self.q = {e: [] for e in ENGS}
        self.res = {}
        self.nops = 0

    def add(self, eng, fn, reads=(), writes=(), dma=None):
        self.nops += 1
        op = Op(eng, fn, dma, self.nops)
        deps = {}
        rkeys = set()
        for a in reads:
            rkeys.update(a.res if isinstance(a, Acc) else a)
        wkeys = set()
        for a in writes:
            wkeys.update(a.res if isinstance(a, Acc) else a)
        res = self.res
        for k in rkeys:
            st = res.get(k)
            if st is not None and st[0] is not None:
                deps[id(st[0])] = st[0]
        for k in wkeys:
            st = res.get(k)
            if st is not None:
                if st[0] is not None:
                    deps[id(st[0])] = st[0]
                for o in st[1].values():
                    deps[id(o)] = o
                for o in st[2]:
                    deps[id(o)] = o
        for k in rkeys:
            if k in wkeys:
                continue
            st = res.get(k)
            if st is None:
                st = res[k] = [None, {}, []]
            if dma is not None:
                st[2].append(op)
            else:
                st[1][eng] = op
        for k in wkeys:
            res[k] = [op, {}, []]
        deps.pop(id(op), None)
        latest = {}
        for d in deps.values():
            if d.dma is not None:
                op.deps[id(d)] = d
            elif d.eng == eng and not (SAME_ENG_SYNC and eng in ("act", "dve")):
                continue
            else:
                o = latest.get(d.eng)
                if o is None or o.seq < d.seq:
                    latest[d.eng] = d
        for d in latest.values():
            op.deps[id(d)] = d
        self.q[eng].append(op)
        return op

    def finalize(self):
        for e in ENGS:
            for op in self.q[e]:
                for d in op.deps.values():
                    d.signal = True
        ecount = {e: 0 for e in ENGS}
        dcount = {}
        for e in ENGS:
            for op in self.q[e]:
                if op.dma is not None:
                    dcount[op.dma] = dcount.get(op.dma, 0) + 16
                    op.ev = (op.dma, dcount[op.dma])
                    op.signal = True
                elif op.signal:
                    ecount[e] += 1
                    op.ev = (e, ecount[e])
        return sorted(dcount.keys())

    def emit(self, eng, h, sems):
        waited = {}
        for op in self.q[eng]:
            need = {}
            for d in op.deps.values():
                if not d.signal:
                    continue
                k, v = d.ev
                if need.get(k, 0) < v:
                    need[k] = v
            for k, v in need.items():
                if waited.get(k, 0) < v:
                    h.wait_ge(sems[k], v)
                    waited[k] = v
            ins = op.fn(h)
            if op.signal:
                k, v = op.ev
                ins.then_inc(sems[k], 16 if op.dma is not None else 1)


class WRef:
    def __init__(self, name, flat_ap, numel, kp):
        self.name, self.flat, self.numel, self.kp = name, flat_ap, numel, kp
        self.offs = {}
        self.order = []
        self.next = 0

    def _one(self, key):
        if key not in self.offs:
            r0, nk, c0, ncols = key
            self.offs[key] = self.next
            self.order.append(key)
            self.next += 128 * nk * ncols
            assert self.next <= self.numel, (self.name, key, self.next, self.numel)
        return self.offs[key]

    def panel(self, r0, nk, c0, ncols):
        if nk > self.kp and nk % self.kp == 0:
            nb = nk // self.kp
            offs = [self._one((r0 + b * self.kp, self.kp, c0, ncols)) for b in range(nb)]
            n1 = 128 * self.kp * ncols
            assert all(offs[b] == offs[0] + b * n1 for b in range(nb)), (self.name, r0, nk, c0, ncols, offs)
            return self.flat[offs[0]:offs[0] + nb * n1].rearrange("(b p e) -> p b e", b=nb, p=128), nb
        o = self._one((r0, nk, c0, ncols))
        return self.flat[o:o + 128 * nk * ncols].rearrange("(p e) -> p e", p=128), 1


class View:
    def __init__(self, arena_ap, off, dtype, n0, n1):
        self.esz = 4 if dtype == F32 else 2
        self.off, self.n0, self.n1 = off, n0, n1
        self.nbytes = n0 * n1 * self.esz
        self.ap = arena_ap[:, off:off + self.nbytes].bitcast(dtype).rearrange("p (a b) -> p a b", b=n1)

    def _res(self, c0, c1, lo, hi):
        b0 = self.off + (c0 * self.n1 + lo) * self.esz
        b1 = self.off + ((c1 - 1) * self.n1 + hi) * self.esz
        return [("sb", g) for g in range(b0 // GRAN, (b1 - 1) // GRAN + 1)]

    def c(self, c, lo=0, hi=None, p0=0, p1=128):
        hi = self.n1 if hi is None else hi
        return Acc(self.ap[p0:p1, c, lo:hi], self._res(c, c + 1, lo, hi))

    def cs(self, c0, c1, lo=0, hi=None, p0=0, p1=128):
        hi = self.n1 if hi is None else hi
        if lo == 0 and hi == self.n1:
            res = self._res(c0, c1, lo, hi)
        else:
            res = []
            for cc in range(c0, c1):
                res.extend(self._res(cc, cc + 1, lo, hi))
        return Acc(self.ap[p0:p1, c0:c1, lo:hi], res)


class Builder:
    def __init__(self, cfg):
        self.cfg = cfg
        self.S = Sched()
        self.nc = bass.Bass("TRN2", target_bir_lowering=False)
        self.bank_rr = 0
        self.pinned = set()
        self.panel_i = 0
        self.evac_i = 0
        self.stg_i = 0
        self.mod_pending = []
        self.wide = False
        self.mod_rate = 0.0
        self.mod_acc = 0.0

    def declare(self):
        c, nc = self.cfg, self.nc
        D, F = c.D, c.F

        def inp(name, shape):
            return nc.dram_tensor(name, list(shape), F32, kind="ExternalInput").ap()

        self.xs = inp("xs", [c.TPC + c.HALO, D])
        self.vecs = inp("vecs", [c.NVB * 128, 128])
        self.ident_d = inp("ident", [128, 128])
        self.hm_d = inp("hm", [128, c.NT])
        self.icnt_d = inp("icnt", [128, c.NT * 4 * 16])
        def winp(name, numel):
            return WRef(name, inp(name, [numel]), numel, c.KP)

        self.w_ada = winp("w_ada", D * 6 * D)
        self.w_in = winp("w_in", D * 4 * D)
        self.w_conv_out = winp("w_conv_out", c.CW * D)
        self.w_pg = winp("w_pool_group", 4 * c.PGS * c.PGS)
        self.w_pool_out = winp("w_pool_out", c.CW * D)
        self.w_o = winp("w_o", D * D)
        self.w_gu = winp("w_gate_up", D * 2 * F)
        self.w_down = winp("w_down", F * D)
        self.wrefs = [self.w_ada, self.w_in, self.w_conv_out, self.w_pg, self.w_pool_out, self.w_o, self.w_gu,
                      self.w_down]
        self.y = nc.dram_tensor("y", [c.TPC, D], F32, kind="ExternalOutput").ap()

    def alloc(self):
        c, nc = self.cfg, self.nc
        KC, TW, TT = c.KC, c.TW, c.TT
        off = 0

        def take(n):
            nonlocal off
            o = off
            off += (n + 63) // 64 * 64
            return o

        m1_need = 2 * c.CC * TT * 2 + 2 * TW * 4 * 2 + 2 * TT * 4 + 3 * TW * 4 + 64 + 2 * c.CPG * TT * 2
        m2_need = 2 * c.CC * TT * 2 + 2 * 4 * TT * 4
        T_size = max(KC * TW * 4, m1_need, m2_need, c.NVB * 128 * 4)
        MG_size = max(KC * TT * 2, 2 * c.XW * 4 + 2 * 4 * TW * 2 + 2 * TW * 4, 4 * TT * 4 + 4 * TT * 2)
        H_size = max(KC * TW * 2, 2 * c.XW * 4)
        T_off = take(T_size)
        H_off = take(H_size)
        MG_off = take(MG_size)
        W_off = take(c.NS * c.SLOT)
        colv_off = take(c.NVB * 128 * 4)
        modc_off = take(6 * KC * 4)
        amod_off = take(2 * KC * 4)
        cs_off = take(KC * 2)
        ones_off = take(128 * 2)
        modloc_off = take(c.MC * 4)
        ident_off = take(128 * 4)
        rstd_off = take(TW * 4)
        htmp2_off = take(2 * TW * 4)
        hm_off = take(c.NT * 4)
        icnt_off = take(c.NT * 4 * 16 * 4)
        total = off
        arena_t = nc.alloc_sbuf_tensor("arena", [128, total], U8)
        self.arena = A = arena_t[:]
        self.total_sbuf = total

        self.T = View(A, T_off, F32, KC, TW)
        self.H = View(A, H_off, BF16, KC, TW)
        self.MG = View(A, MG_off, BF16, KC, TT)
        self.W_off = W_off
        o = T_off
        self.Z = View(A, o, BF16, c.CC, TT); o += c.CC * TT * 2
        self.MX = View(A, o, BF16, c.CC, TT); o += c.CC * TT * 2
        self.gcs = View(A, o, F32, 2, TW); o += 2 * TW * 4
        self.u = View(A, o, F32, 2, TW); o += 2 * TW * 4
        self.yc = View(A, o, F32, 2, TT); o += 2 * TT * 4
        self.pb = View(A, o, F32, 3, TW); o += 3 * TW * 4
        self.t16 = View(A, o, F32, 1, 16); o += 64
        self.pooled = [View(A, o + i * c.CPG * TT * 2, BF16, c.CPG, TT) for i in range(2)]
        o += 2 * c.CPG * TT * 2
        m1_end = o
        o = T_off + 2 * c.CC * TT * 2
        self.sgA = View(A, o, F32, 4, TT); o += 4 * TT * 4
        self.sgB = View(A, o, F32, 4, TT); o += 4 * TT * 4
        assert max(o, m1_end) <= T_off + T_size, (o, m1_end, T_off + T_size)
        self.vstg = View(A, T_off, F32, c.NVB, 128)
        o = MG_off
        self.stgA = [View(A, o + i * c.XW * 4, F32, 1, c.XW) for i in range(2)]
        o += 2 * c.XW * 4
        self.sq4 = [View(A, o + i * 4 * TW * 2, BF16, 4, TW) for i in range(2)]
        o += 2 * 4 * TW * 2
        self.htmp = [View(A, o + i * TW * 4, F32, 1, TW) for i in range(2)]
        self.htmp += [View(A, htmp2_off + i * TW * 4, F32, 1, TW) for i in range(2)]
        o += 2 * TW * 4
        assert o <= MG_off + MG_size, (o - MG_off, MG_size)
        o = MG_off
        self.sgt = View(A, o, F32, 4, TT); o += 4 * TT * 4
        self.actb = View(A, o, BF16, 4, TT); o += 4 * TT * 2
        assert o <= MG_off + MG_size
        self.stgH = [View(A, H_off + i * c.XW * 4, F32, 1, c.XW) for i in range(2)]
        self.colv = View(A, colv_off, F32, 1, c.NVB * 128)
        self.modc = View(A, modc_off, F32, 1, 6 * KC)
        self.amod = View(A, amod_off, F32, 1, 2 * KC)
        self.csb = View(A, cs_off, BF16, 1, KC)
        self.ones = View(A, ones_off, BF16, 1, 128)
        self.modloc = View(A, modloc_off, F32, 1, c.MC)
        self.ident = View(A, ident_off, F32, 1, 128)
        self.rstd = View(A, rstd_off, F32, 1, TW)
        self.hm = View(A, hm_off, F32, 1, c.NT)
        self.icnt = View(A, icnt_off, F32, 1, c.NT * 4 * 16)
        self.ps = [nc.alloc_psum_tensor("ps%d" % b, [128, 512], F32) for b in range(8)]

    def bank(self):
        while True:
            b = self.bank_rr
            self.bank_rr = (self.bank_rr + 1) % 8
            if b not in self.pinned:
                return b

    def pacc(self, b, lo=0, hi=512, p0=0, p1=128):
        return Acc(self.ps[b][p0:p1, lo:hi], [("ps", b)])

    def pacc3(self, b, n, w, lo=0, hi=None, p0=0, p1=128):
        hi = w if hi is None else hi
        ap = self.ps[b][p0:p1, 0:n * w].rearrange("p (a b) -> p a b", b=w)[:, :, lo:hi]
        return Acc(ap, [("ps", b)])

    def colcol(self, idx):
        return self.colv.c(0, idx, idx + 1)

    def load_panel(self, dram2d, r0, nk, c0, ncols):
        c = self.cfg
        assert nk * ncols * 2 <= 2 * c.SLOT
        nsl = 1 if nk * ncols * 2 <= c.SLOT else 2
        s = self.panel_i % c.NS
        if nsl == 2 and s == c.NS - 1:
            self.panel_i += 1
            s = 0
        self.panel_i += nsl
        v = View(self.arena, self.W_off + s * c.SLOT, BF16, nk, ncols)
        src, nb = dram2d.panel(r0, nk, c0, ncols)
        dst = v.cs(0, nk)
        v2 = self.arena[:, v.off:v.off + nk * ncols * 2].bitcast(BF16)
        if nb > 1:
            v2 = v2.rearrange("p (b e) -> p b e", b=nb)
        self.S.add("pool", lambda e, o=v2, i=src: e.dma_start(out=o, in_=i),
                   writes=[dst], dma="w%d" % s)
        return v

    def mm(self, out, lhsT, rhs, start, stop, **kw):
        self.S.add("pe", lambda e, o=out.ap, l=lhsT.ap, r=rhs.ap: e.matmul(o, l, r, start=start, stop=stop, **kw),
                   reads=[lhsT, rhs], writes=[out])

    def act(self, out, in_, func, bias=None, scale=None, extra_reads=()):
        kw = {}
        rd = [in_] + list(extra_reads)
        if bias is not None:
            if isinstance(bias, Acc):
                kw["bias"] = bias.ap
                rd.append(bias)
            else:
                kw["bias"] = bias
        if scale is not None:
            if isinstance(scale, Acc):
                kw["scale"] = scale.ap
                rd.append(scale)
            else:
                kw["scale"] = scale
        self.S.add("act", lambda e, o=out.ap, i=in_.ap: e.activation(o, i, func, **kw), reads=rd, writes=[out])

    def tt(self, out, a, b, op):
        self.S.add("dve", lambda e, o=out.ap, x=a.ap, y=b.ap: e.tensor_tensor(o, x, y, op),
                   reads=[a, b], writes=[out])

    def stt(self, out, in0, scalar, in1, op0, op1):
        rd = [in0, in1]
        if isinstance(scalar, Acc):
            rd.append(scalar)
            sc = scalar.ap
        else:
            sc = scalar
        self.S.add("dve", lambda e, o=out.ap, x=in0.ap, y=in1.ap: e.scalar_tensor_tensor(o, x, sc, y, op0, op1),
                   reads=rd, writes=[out])

    def ts(self, out, in0, s1, op0):
        rd = [in0]
        if isinstance(s1, Acc):
            rd.append(s1)
            sc = s1.ap
        else:
            sc = s1
        self.S.add("dve", lambda e, o=out.ap, x=in0.ap: e.tensor_scalar(o, x, sc, None, op0), reads=rd, writes=[out])

    def copy(self, out, in_, eng=None):
        if eng is None:
            eng = "act" if self.evac_i % 2 == 0 else "dve"
            self.evac_i += 1
        if eng == "act":
            self.S.add("act", lambda e, o=out.ap, i=in_.ap: e.copy(o, i), reads=[in_], writes=[out])
        else:
            self.S.add("dve", lambda e, o=out.ap, i=in_.ap: e.tensor_copy(o, i), reads=[in_], writes=[out])

    def proj(self, wd, k0, nk_total, col0, nch, src, pieces, kp=None):
        c = self.cfg
        if kp is None and self.wide and nk_total == c.KC and nch == 4:
            kp = c.KC
        KP = min(c.KP if kp is None else kp, nk_total)
        banks = [[self.bank() for _ in pieces] for _ in range(nch)]
        for half in range(nk_total // KP):
            pan = self.load_panel(wd, k0 + half * KP, KP, col0, nch * 128)
            for ch in range(nch):
                for kc in range(KP):
                    kg = half * KP + kc
                    for pi, (lo, hi) in enumerate(pieces):
                        self.mm(self.pacc(banks[ch][pi], 0, hi - lo), pan.c(kc, ch * 128, (ch + 1) * 128),
                                src.c(kg, lo, hi), kg == 0, kg == nk_total - 1)
            for _ in range(max(1, KP // c.KP)):
                self.mod_tick()
        return banks

    def mod_unit(self, col0, ncols):
        c = self.cfg
        pan = self.load_panel(self.w_ada, 0, c.KC, col0, ncols)
        for j in range(ncols // 128):
            col = col0 // 128 + j
            for kg in range(c.KC):
                self.mm(self.pacc(self.mb, col, col + 1), pan.c(kg, j * 128, (j + 1) * 128),
                        self.csb.c(0, kg, kg + 1), False, False, skip_group_check=True)

    def mod_tick(self, flush=False):
        if not self.mod_pending:
            return
        self.mod_acc += self.mod_rate
        while self.mod_pending and (flush or self.mod_acc >= 1.0):
            self.mod_acc -= 1.0
            self.mod_unit(*self.mod_pending.pop(0))

    def mod_finalize(self, m0, m1):
        KC = self.cfg.KC
        self.tt(self.modc.c(0, m0, m1), self.pacc(self.mb, m0, m1), self.colv.c(0, KC + m0, KC + m1), ALU.add)

    def load_x_T(self, t, stg, with_halo):
        c = self.cfg
        blocks = []
        if with_halo:
            blocks.append((t * c.TT, 16, 0))
        for j in range(c.TT // 128):
            blocks.append((t * c.TT + c.HALO + j * 128, 128, c.HALO + j * 128))
        nh = c.D // c.XW
        qn = c.XW // 128
        for (r0, nt, w0) in blocks:
            for hh in range(nh):
                sv = stg[self.stg_i % 2]
                self.stg_i += 1
                d = sv.c(0, 0, c.XW, 0, nt)
                src = self.xs[r0:r0 + nt, hh * c.XW:(hh + 1) * c.XW]
                self.S.add("sp", lambda e, o=d.ap, i=src: e.dma_start(out=o, in_=i), writes=[d],
                           dma="xs%d" % ((self.stg_i - 1) % 2))
                for q4 in range(qn // 4):
                    b = self.bank()
                    for qq in range(4):
                        q = q4 * 4 + qq
                        o = self.pacc(b, qq * 128, qq * 128 + nt)
                        i = sv.c(0, q * 128, (q + 1) * 128, 0, nt)
                        idn = self.ident.c(0, 0, nt, 0, nt)
                        self.S.add("pe", lambda e, o=o.ap, i=i.ap, d=idn.ap: e.transpose(o, i, d),
                                   reads=[i, idn], writes=[o])
                    c0 = hh * qn + q4 * 4
                    self.copy(self.T.cs(c0, c0 + 4, w0, w0 + nt), self.pacc3(b, 4, 128, 0, nt))

    def rms_rstd(self, pieces):
        c = self.cfg
        banks = [self.bank() for _ in pieces]
        for b in banks:
            self.pinned.add(b)
        lo_a = min(p[0] for p in pieces)
        hi_a = max(p[1] for p in pieces)
        for c4 in range(0, c.KC, 4):
            sq = self.sq4[(c4 // 4) % 2]
            self.act(sq.cs(0, 4, lo_a, hi_a), self.T.cs(c4, c4 + 4, lo_a, hi_a), AF.Square)
            for j in range(4):
                ch = c4 + j
                for pi, (lo, hi) in enumerate(pieces):
                    self.mm(self.pacc(banks[pi], 0, hi - lo), self.ones.c(0), sq.c(j, lo, hi), ch == 0, ch == c.KC - 1)
        for pi, (lo, hi) in enumerate(pieces):
            self.act(self.rstd.c(0, lo, hi), self.pacc(banks[pi], 0, hi - lo), AF.Sqrt, bias=self.epsb, scale=1.0 / c.D)
            r = self.rstd.c(0, lo, hi)
            self.S.add("dve", lambda e, o=r.ap: e.reciprocal(o, o), reads=[r], writes=[r])
        for b in banks:
            self.pinned.discard(b)

    def norm_to_h(self, pieces, a0, s0, stats=True):
        c = self.cfg
        if stats:
            self.rms_rstd(pieces)
        lo = min(p[0] for p in pieces)
        hi = max(p[1] for p in pieces)
        for ch in range(c.KC):
            tmp = self.htmp[ch % 4]
            self.tt(tmp.c(0, lo, hi), self.T.c(ch, lo, hi), self.rstd.c(0, lo, hi), ALU.mult)
            self.act(self.H.c(ch, lo, hi), tmp.c(0, lo, hi), AF.Identity,
                     bias=self.modc.c(0, s0 + ch, s0 + ch + 1), scale=self.amod.c(0, a0 + ch, a0 + ch + 1))

    def setup(self):
        c = self.cfg
        KC = c.KC
        S = self.S
        v = self.vstg.cs(0, c.NVB)
        S.add("sp", lambda e, o=v.ap, i=self.vecs.rearrange("(b p) f -> p b f", p=128): e.dma_start(out=o, in_=i),
              writes=[v], dma="c0")
        idn = self.ident.c(0)
        S.add("sp", lambda e, o=idn.ap, i=self.ident_d: e.dma_start(out=o, in_=i), writes=[idn], dma="c1")
        hm = self.hm.c(0)
        S.add("sp", lambda e, o=hm.ap, i=self.hm_d: e.dma_start(out=o, in_=i), writes=[hm], dma="c2")
        ic = self.icnt.c(0)
        S.add("sp", lambda e, o=ic.ap, i=self.icnt_d: e.dma_start(out=o, in_=i), writes=[ic], dma="c3")
        on = self.ones.c(0)
        S.add("dve", lambda e, o=on.ap: e.memset(o, 1.0), writes=[on])
        b = self.bank()
        for blk in range(c.NVB):
            o = self.pacc(b, blk * 128, (blk + 1) * 128)
            i = self.vstg.c(blk)
            S.add("pe", lambda e, o=o.ap, i=i.ap, d=idn.ap: e.transpose(o, i, d), reads=[i, idn], writes=[o])
        self.copy(self.colv.c(0), self.pacc(b, 0, c.NVB * 128), eng="dve")
        self.act(self.csb.c(0), self.colv.c(0, 0, KC), AF.Silu)
        self.epsb = self.amod_eps()
        halves = [(0, c.TW // 2), (c.TW // 2, c.TW)]
        self.load_x_T(0, self.stgA, True)
        self.rms_rstd(halves)
        self.mb = mb = self.bank()
        self.pinned.add(mb)
        mp = self.pacc(mb)
        S.add("dve", lambda e, o=mp.ap: e.memset(o, 0.0), writes=[mp])
        units = []
        col0 = 0
        while col0 < 6 * c.D:
            ncols = min(512, 6 * c.D - col0)
            units.append((col0, ncols))
            col0 += ncols
        first = [u for u in units if u[0] < 2 * c.D]
        self.mod_pending = [u for u in units if u[0] >= 2 * c.D]
        for u in first:
            self.mod_unit(*u)
        self.mod_finalize(0, 2 * KC)
        self.stt(self.amod.c(0, 0, KC), self.modc.c(0, KC, 2 * KC), 1.0, self.colv.c(0, 7 * KC, 8 * KC), ALU.add, ALU.mult)
        npan = (c.CC // 2) * 4 * (KC // c.KP) + (c.D // 512) * (2 * (KC // c.KP) + 2 * max(1, c.CC // c.KP))
        self.mod_rate = len(self.mod_pending) / (0.9 * npan)

    def mod_rest(self):
        c = self.cfg
        KC = c.KC
        self.mod_tick(flush=True)
        self.mod_finalize(2 * KC, 6 * KC)
        self.pinned.discard(self.mb)
        self.stt(self.amod.c(0, KC, 2 * KC), self.modc.c(0, 4 * KC, 5 * KC), 1.0, self.colv.c(0, 8 * KC, 9 * KC),
                 ALU.add, ALU.mult)

    def amod_eps(self):
        c = self.cfg
        if c.NVB * 128 > c.NV:
            a = self.colv.c(0, c.NV, c.NV + 1)
        else:
            raise AssertionError("no room for eps column")
        self.S.add("dve", lambda e, o=a.ap: e.memset(o, EPS), reads=[self.colv.c(0, 0, 1)], writes=[a])
        return a

    def tile(self, t):
        c = self.cfg
        S = self.S
        KC, CC, TT, TW, H0 = c.KC, c.CC, c.TT, c.TW, c.HALO
        own = [(H0, TW)]
        halves = [(0, TW // 2), (TW // 2, TW)]
        V0 = KC
        SH1, SC1, G1, SH2, SC2, G2 = 0, KC, 2 * KC, 3 * KC, 4 * KC, 5 * KC
        CV_BGC, CV_BGP, CV_CW, CV_PS, CV_GF = 10 * KC, 11 * KC, 12 * KC, 12 * KC + 3 * CC, 9 * KC

        if t == 0:
            self.norm_to_h(halves, 0, SH1, stats=False)
        else:
            self.load_x_T(t, self.stgA, True)
            self.norm_to_h(halves, 0, SH1)

        hmcol = self.hm.c(0, t, t + 1)
        for cp in range(CC // 2):
            bg = self.proj(self.w_in, 0, KC, c.CW + cp * 256, 2, self.H, halves, kp=KC)
            for ch in range(2):
                for pi, (lo, hi) in enumerate(halves):
                    self.copy(self.gcs.c(ch, lo, hi), self.pacc(bg[ch][pi], 0, hi - lo), eng="act")
            bv = self.proj(self.w_in, 0, KC, 2 * c.CW + cp * 256, 2, self.H, halves, kp=KC)
            for ch in range(2):
                cg_ = cp * 2 + ch
                for pi, (lo, hi) in enumerate(halves):
                    self.tt(self.u.c(ch, lo, hi), self.gcs.c(ch, lo, hi), self.pacc(bv[ch][pi], 0, hi - lo), ALU.mult)
                self.ts(self.u.c(ch, 0, H0), self.u.c(ch, 0, H0), hmcol, ALU.mult)
                w0 = self.colcol(CV_CW + 0 * CC + cg_)
                w1 = self.colcol(CV_CW + 1 * CC + cg_)
                w2 = self.colcol(CV_CW + 2 * CC + cg_)
                self.ts(self.yc.c(ch), self.u.c(ch, H0, TW), w2, ALU.mult)
                self.stt(self.yc.c(ch), self.u.c(ch, H0 - 1, TW - 1), w1, self.yc.c(ch), ALU.mult, ALU.add)
                self.stt(self.yc.c(ch), self.u.c(ch, H0 - 2, TW - 2), w0, self.yc.c(ch), ALU.mult, ALU.add)
            bb = self.proj(self.w_in, 0, KC, 0 + cp * 256, 2, self.H, own, kp=KC)
            for ch in range(2):
                self.tt(self.Z.c(cp * 2 + ch), self.yc.c(ch), self.pacc(bb[ch][0]), ALU.mult)

        for cp in range(CC // 2):
            bp = self.proj(self.w_in, 0, KC, 3 * c.CW + cp * 256, 2, self.H, halves, kp=KC)
            for ch in range(2):
                pc = cp * 2 + ch
                g = pc // c.CPG
                kin = pc % c.CPG
                w = POOL_WINDOWS[g]
                p0 = self.pb
                for pi, (lo, hi) in enumerate(halves):
                    self.copy(p0.c(0, lo, hi), self.pacc(bp[ch][pi], 0, hi - lo), eng="act")
                self.ts(p0.c(0, 0, H0), p0.c(0, 0, H0), hmcol, ALU.mult)
                src = 0
                for i in range(g + 1):
                    sh = 1 << i
                    st = (1 << (i + 1)) - 1
                    dst = 1 + (i % 2)
                    self.tt(self.pb.c(dst, st, TW), self.pb.c(src, st, TW), self.pb.c(src, st - sh, TW - sh), ALU.add)
                    src = dst
                pl = self.pooled[g % 2]
                ic = self.icnt.c(0, (t * 4 + g) * 16, (t * 4 + g) * 16 + 16)
                self.tt(self.t16.c(0), self.pb.c(src, H0, H0 + 16), ic, ALU.mult)
                self.tt(pl.c(kin, 0, 16), self.t16.c(0), p0.c(0, H0, H0 + 16), ALU.subtract)
                self.stt(pl.c(kin, 16, TT), self.pb.c(src, H0 + 16, TW), 1.0 / w, p0.c(0, H0 + 16, TW),
                         ALU.mult, ALU.subtract)
                if kin == c.CPG - 1:
                    pan = self.load_panel(self.w_pg, g * c.CPG, c.CPG, 0, c.PGS)
                    for e_ in range(c.CPG):
                        b = self.bank()
                        for kc in range(c.CPG):
                            self.mm(self.pacc(b), pan.c(kc, e_ * 128, (e_ + 1) * 128), pl.c(kc), kc == 0, kc == c.CPG - 1)
                        mc = g * c.CPG + e_
                        self.act(self.MX.c(mc), self.pacc(b), AF.Copy, scale=self.colcol(CV_PS + mc))

        self.wide = (t == 0)
        for cg in range(c.D // 512):
            bgc = self.proj(self.w_in, 0, KC, 4 * c.CW + cg * 512, 4, self.H, own)
            for j in range(4):
                self.act(self.sgA.c(j), self.pacc(bgc[j][0]), AF.Sigmoid, bias=self.colcol(CV_BGC + cg * 4 + j))
            bgp = self.proj(self.w_in, 0, KC, 4 * c.CW + c.D + cg * 512, 4, self.H, own)
            for j in range(4):
                self.act(self.sgB.c(j), self.pacc(bgp[j][0]), AF.Sigmoid, bias=self.colcol(CV_BGP + cg * 4 + j))
            byc = self.proj(self.w_conv_out, 0, CC, cg * 512, 4, self.Z, [(0, TT)])
            for j in range(4):
                self.tt(self.sgA.c(j), self.sgA.c(j), self.pacc(byc[j][0]), ALU.mult)
            byp = self.proj(self.w_pool_out, 0, CC, cg * 512, 4, self.MX, [(0, TT)])
            for j in range(4):
                self.tt(self.sgB.c(j), self.sgB.c(j), self.pacc(byp[j][0]), ALU.mult)
                self.tt(self.MG.c(cg * 4 + j), self.sgA.c(j), self.sgB.c(j), ALU.add)

        if t == 0:
            self.mod_rest()
        self.wide = False
        self.load_x_T(t, self.stgH, False)
        for cg in range(c.D // 512):
            bo = self.proj(self.w_o, 0, KC, cg * 512, 4, self.MG, [(0, TT)])
            for j in range(4):
                n = cg * 4 + j
                self.stt(self.T.c(n, H0, TW), self.pacc(bo[j][0]), self.modc.c(0, G1 + n, G1 + n + 1),
                         self.T.c(n, H0, TW), ALU.mult, ALU.add)

        self.norm_to_h(own, KC, SH2)

        nfg = (c.FC + 3) // 4
        for fg in range(nfg):
            ncf = min(4, c.FC - 4 * fg)
            bgt = self.proj(self.w_gu, 0, KC, fg * 512, ncf, self.H, own)
            for j in range(ncf):
                self.act(self.sgt.c(j), self.pacc(bgt[j][0]), AF.Silu)
            bup = self.proj(self.w_gu, 0, KC, c.F + fg * 512, ncf, self.H, own)
            for j in range(ncf):
                self.tt(self.actb.c(j), self.sgt.c(j), self.pacc(bup[j][0]), ALU.mult)
            for hc in range(c.D // c.DW):
                pan = self.load_panel(self.w_down, fg * 4, ncf, hc * c.DW, c.DW)
                for j in range(c.DW // 128):
                    n = hc * (c.DW // 128) + j
                    b = self.bank()
                    for fc in range(ncf):
                        self.mm(self.pacc(b), pan.c(fc, j * 128, (j + 1) * 128), self.actb.c(fc), fc == 0, fc == ncf - 1)
                    self.stt(self.T.c(n, H0, TW), self.pacc(b), self.modc.c(0, G2 + n, G2 + n + 1),
                             self.T.c(n, H0, TW), ALU.mult, ALU.add)

        self.rms_rstd(own)
        for ch in range(KC):
            self.stt(self.T.c(ch, H0, TW), self.T.c(ch, H0, TW), self.colcol(CV_GF + ch), self.rstd.c(0, H0, TW),
                     ALU.mult, ALU.mult)
        qn = c.XW // 128
        idn = self.ident.c(0)
        for jb in range(TT // 128):
            w0 = H0 + jb * 128
            for hh in range(c.D // c.XW):
                sv = self.stgH[self.stg_i % 2]
                si = self.stg_i % 2
                self.stg_i += 1
                for q4 in range(qn // 4):
                    b = self.bank()
                    for qq in range(4):
                        ch = hh * qn + q4 * 4 + qq
                        o = self.pacc(b, qq * 128, (qq + 1) * 128)
                        i = self.T.c(ch, w0, w0 + 128)
                        S.add("pe", lambda e, o=o.ap, i=i.ap, d=idn.ap: e.transpose(o, i, d), reads=[i, idn], writes=[o])
                    self.copy(sv.c(0, q4 * 512, (q4 + 1) * 512), self.pacc(b))
                s_ = sv.c(0)
                r0 = t * TT + jb * 128
                dst = self.y[r0:r0 + 128, hh * c.XW:(hh + 1) * c.XW]
                S.add("sp", lambda e, o=dst, i=s_.ap: e.dma_start(out=o, in_=i), reads=[s_],
                      writes=[[("y", t, jb, hh)]], dma="st%d" % si)

    def finish(self):
        c = self.cfg
        allw = [[("y", t, jb, hh)] for t in range(c.NT) for jb in range(c.TT // 128) for hh in range(c.D // c.XW)]
        self.S.add("sp", lambda e: e.nop(), reads=allw)

    def build(self):
        self.declare()
        self.alloc()
        self.setup()
        for t in range(self.cfg.NT):
            self.tile(t)
        self.finish()
        nc = self.nc
        S = self.S
        dkeys = S.finalize()
        with contextlib.ExitStack() as es:
            sems = {}
            for k in list(ENGS) + dkeys:
                sems[k] = es.enter_context(nc.semaphore("s_" + k))
            block = es.enter_context(nc.Block())

            @block.tensor
            def _(e):
                S.emit("pe", e, sems)

            @block.scalar
            def _(e):
                S.emit("act", e, sems)

            @block.vector
            def _(e):
                S.emit("dve", e, sems)

            @block.gpsimd
            def _(e):
                S.emit("pool", e, sems)

            @block.sync
            def _(e):
                S.emit("sp", e, sems)
        return nc


def host_inputs(cfg, specs, x, c, w_ada, b_ada, norm1_gain, w_in, b_branch_gate, conv_w, w_conv_out, w_pool_group,
                pool_scale, w_pool_out, w_o, norm2_gain, w_gate_up, w_down, final_norm_gain):
    f = lambda a: np.ascontiguousarray(np.asarray(a, dtype=np.float32))
    D = cfg.D
    x2 = f(x).reshape(cfg.S, D)
    rows = [f(c).reshape(-1), f(b_ada).reshape(-1), f(norm1_gain).reshape(-1), f(norm2_gain).reshape(-1),
            f(final_norm_gain).reshape(-1), f(b_branch_gate).reshape(-1), f(conv_w).reshape(-1), f(pool_scale).reshape(-1)]
    vec = np.concatenate(rows)
    assert vec.size == cfg.NV * 128
    vecs = np.zeros((cfg.NVB * 128, 128), np.float32)
    vecs.reshape(-1)[:vec.size] = vec
    ident = np.eye(128, dtype=np.float32)
    mats = {
        "w_ada": f(w_ada).reshape(D, 6 * D), "w_in": f(w_in).reshape(D, 4 * D),
        "w_conv_out": f(w_conv_out).reshape(cfg.CW, D), "w_pool_group": f(w_pool_group).reshape(4 * cfg.PGS, cfg.PGS),
        "w_pool_out": f(w_pool_out).reshape(cfg.CW, D), "w_o": f(w_o).reshape(D, D),
        "w_gate_up": f(w_gate_up).reshape(D, 2 * cfg.F), "w_down": f(w_down).reshape(cfg.F, D),
    }
    shared = {"vecs": vecs, "ident": ident}
    for name, order in specs.items():
        W = mats[name]
        flat = np.empty(W.size, np.float32)
        o = 0
        for (r0, nk, c0, ncols) in order:
            blk = W[r0 * 128:(r0 + nk) * 128, c0:c0 + ncols].reshape(nk, 128, ncols)
            n = 128 * nk * ncols
            flat[o:o + n].reshape(128, nk, ncols)[...] = blk.transpose(1, 0, 2)
            o += n
        assert o == W.size, (name, o, W.size)
        shared[name] = flat
    xpad = np.concatenate([np.zeros((cfg.HALO, D), np.float32), x2], axis=0)
    in_maps = []
    for core in range(cfg.NCORES):
        m = dict(shared)
        m["xs"] = np.ascontiguousarray(xpad[core * cfg.TPC: (core + 1) * cfg.TPC + cfg.HALO])
        hm = np.ones((128, cfg.NT), np.float32)
        icnt = np.zeros((cfg.NT, 4, 16), np.float32)
        for t in range(cfg.NT):
            g0 = core * cfg.TPC + t * cfg.TT
            if g0 == 0:
                hm[:, t] = 0.0
            for g, w in enumerate(POOL_WINDOWS):
                pos = g0 + np.arange(16)
                icnt[t, g] = 1.0 / np.minimum(pos + 1, w)
        m["hm"] = hm
        m["icnt"] = np.ascontiguousarray(np.broadcast_to(icnt.reshape(1, -1), (128, cfg.NT * 64)))
        in_maps.append(m)
    return in_maps


_NC_CACHE = {}


def run(cfg, inputs):
    key = (cfg.D, cfg.F, cfg.S, cfg.KP, cfg.NS)
    if key not in _NC_CACHE:
        b = Builder(cfg)
        nc = b.build()
        _NC_CACHE[key] = (nc, {w.name: list(w.order) for w in b.wrefs})
    nc, specs = _NC_CACHE[key]
    in_maps = host_inputs(cfg, specs, **inputs)
    res = run_bass_kernel_spmd(nc, in_maps, core_ids=list(range(cfg.NCORES)))
    out = np.concatenate([np.asarray(r["y"]) for r in res.results], axis=0)
    return out.reshape(1, cfg.S, cfg.D).astype(np.float32)


def kernel(**inputs):
    cfg = Cfg()
    return run(cfg, inputs)
```

```python
import contextlib
import numpy as np
import concourse.bass as bass
import concourse.mybir as mybir
from concourse.bass_utils import run_bass_kernel_spmd

F32 = mybir.dt.float32
BF16 = mybir.dt.bfloat16
U8 = mybir.dt.uint8
AF = mybir.ActivationFunctionType
ALU = mybir.AluOpType

POOL_WINDOWS = (2, 4, 8, 16)
EPS = 1e-6
GRAN = 16
SAME_ENG_SYNC = True


class Cfg:
    def __init__(self, D=4096, F=11008, S=8192, ncores=8, KP=16, NS=4):
        self.D, self.F, self.S, self.NCORES, self.KP, self.NS = D, F, S, ncores, KP, NS
        self.KC = D // 128
        self.CW = D // 2
        self.CC = self.CW // 128
        self.PGS = self.CW // 4
        self.CPG = self.PGS // 128
        self.FC = F // 128
        self.TPC = S // ncores
        self.TT = 512
        self.NT = self.TPC // self.TT
        self.HALO = 16
        self.TW = self.TT + self.HALO
        self.XW = min(D, 2048)
        self.DW = min(D, 2048)
        self.NV = 14 * self.KC
        self.NVB = (self.NV + 127) // 128
        self.SLOT = 16384
        self.MC = 6 * self.KC
        assert self.KC % KP == 0 and (self.CC % KP == 0 or self.CC <= KP)


class Acc:
    __slots__ = ("ap", "res")

    def __init__(self, ap, res):
        self.ap = ap
        self.res = res


class Op:
    __slots__ = ("eng", "fn", "deps", "ev", "signal", "dma", "seq")

    def __init__(self, eng, fn, dma, seq):
        self.eng, self.fn, self.dma, self.seq = eng, fn, dma, seq
        self.deps = {}
        self.ev = None
        self.signal = False


ENGS = ("pe", "act", "dve", "pool", "sp")


class Sched:
    def __init__(self):
        self.q = {e: [] for e in ENGS}
        self.res = {}
        self.nops = 0

    def add(self, eng, fn, reads=(), writes=(), dma=None):
        self.nops += 1
        op = Op(eng, fn, dma, self.nops)
        deps = {}
        rkeys = set()
        for a in reads:
            rkeys.update(a.res if isinstance(a, Acc) else a)
        wkeys = set()
        for a in writes:
            wkeys.update(a.res if isinstance(a, Acc) else a)
        res = self.res
        for k in rkeys:
            st = res.get(k)
            if st is not None and st[0] is not None:
                deps[id(st[0])] = st[0]
        for k in wkeys:
            st = res.get(k)
            if st is not None:
                if st[0] is not None:
                    deps[id(st[0])] = st[0]
                for o in st[1].values():
                    deps[id(o)] = o
                for o in st[2]:
                    deps[id(o)] = o
        for k in rkeys:
            if k in wkeys:
                continue
            st = res.get(k)
            if st is None:
                st = res[k] = [None, {}, []]
            if dma is not None:
                st[2].append(op)
            else:
                st[1][eng] = op
        for k in wkeys:
            res[k] = [op, {}, []]
        deps.pop(id(op), None)
        latest = {}
        for d in deps.values():
            if d.dma is not None:
                op.deps[id(d)] = d
            elif d.eng == eng and not (SAME_ENG_SYNC and eng in ("act", "dve")):
                continue
            else:
                o = latest.get(d.eng)
                if o is None or o.seq < d.seq:
                    latest[d.eng] = d
        for d in latest.values():
            op.deps[id(d)] = d
        self.q[eng].append(op)
        return op

    def finalize(self):
        for e in ENGS:
            for op in self.q[e]:
                for d in op.deps.values():
                    d.signal = True
        ecount = {e: 0 for e in ENGS}
        dcount = {}
        for e in ENGS:
            for op in self.q[e]:
                if op.dma is not None:
                    dcount[op.dma] = dcount.get(op.dma, 0) + 16
                    op.ev = (op.dma, dcount[op.dma])
                    op.signal = True
                elif op.signal:
                    ecount[e] += 1
                    op.ev = (e, ecount[e])
        return sorted(dcount.keys())

    def emit(self, eng, h, sems):
        waited = {}
        for op in self.q[eng]:
            need = {}
            for d in op.deps.values():
                if not d.signal:
                    continue
                k, v = d.ev
                if need.get(k, 0) < v:
                    need[k] = v
            for k, v in need.items():
                if waited.get(k, 0) < v:
                    h.wait_ge(sems[k], v)
                    waited[k] = v
            ins = op.fn(h)
            if op.signal:
                k, v = op.ev
                ins.then_inc(sems[k], 16 if op.dma is not None else 1)


class WRef:
    def __init__(self, name, flat_ap, numel):
        self.name, self.flat, self.numel = name, flat_ap, numel
        self.offs = {}
        self.order = []
        self.next = 0

    def panel(self, r0, nk, c0, ncols):
        key = (r0, nk, c0, ncols)
        if key not in self.offs:
            self.offs[key] = self.next
            self.order.append(key)
            self.next += 128 * nk * ncols
            assert self.next <= self.numel, (self.name, key, self.next, self.numel)
        o = self.offs[key]
        return self.flat[o:o + 128 * nk * ncols].rearrange("(p e) -> p e", p=128)


class View:
    def __init__(self, arena_ap, off, dtype, n0, n1):
        self.esz = 4 if dtype == F32 else 2
        self.off, self.n0, self.n1 = off, n0, n1
        self.nbytes = n0 * n1 * self.esz
        self.ap = arena_ap[:, off:off + self.nbytes].bitcast(dtype).rearrange("p (a b) -> p a b", b=n1)

    def _res(self, c0, c1, lo, hi):
        b0 = self.off + (c0 * self.n1 + lo) * self.esz
        b1 = self.off + ((c1 - 1) * self.n1 + hi) * self.esz
        return [("sb", g) for g in range(b0 // GRAN, (b1 - 1) // GRAN + 1)]

    def c(self, c, lo=0, hi=None, p0=0, p1=128):
        hi = self.n1 if hi is None else hi
        return Acc(self.ap[p0:p1, c, lo:hi], self._res(c, c + 1, lo, hi))

    def cs(self, c0, c1, lo=0, hi=None, p0=0, p1=128):
        hi = self.n1 if hi is None else hi
        if lo == 0 and hi == self.n1:
            res = self._res(c0, c1, lo, hi)
        else:
            res = []
            for cc in range(c0, c1):
                res.extend(self._res(cc, cc + 1, lo, hi))
        return Acc(self.ap[p0:p1, c0:c1, lo:hi], res)


class Builder:
    def __init__(self, cfg):
        self.cfg = cfg
        self.S = Sched()
        self.nc = bass.Bass("TRN2", target_bir_lowering=False)
        self.bank_rr = 0
        self.pinned = set()
        self.panel_i = 0
        self.evac_i = 0
        self.stg_i = 0
        self.mod_pending = []
        self.mod_rate = 0.0
        self.mod_acc = 0.0

    def declare(self):
        c, nc = self.cfg, self.nc
        D, F = c.D, c.F

        def inp(name, shape):
            return nc.dram_tensor(name, list(shape), F32, kind="ExternalInput").ap()

        self.xs = inp("xs", [c.TPC + c.HALO, D])
        self.vecs = inp("vecs", [c.NVB * 128, 128])
        self.ident_d = inp("ident", [128, 128])
        self.hm_d = inp("hm", [128, c.NT])
        self.icnt_d = inp("icnt", [128, c.NT * 4 * 16])
        def winp(name, numel):
            return WRef(name, inp(name, [numel]), numel)

        self.w_ada = winp("w_ada", D * 6 * D)
        self.w_in = winp("w_in", D * 4 * D)
        self.w_conv_out = winp("w_conv_out", c.CW * D)
        self.w_pg = winp("w_pool_group", 4 * c.PGS * c.PGS)
        self.w_pool_out = winp("w_pool_out", c.CW * D)
        self.w_o = winp("w_o", D * D)
        self.w_gu = winp("w_gate_up", D * 2 * F)
        self.w_down = winp("w_down", F * D)
        self.wrefs = [self.w_ada, self.w_in, self.w_conv_out, self.w_pg, self.w_pool_out, self.w_o, self.w_gu,
                      self.w_down]
        self.y = nc.dram_tensor("y", [c.TPC, D], F32, kind="ExternalOutput").ap()

    def alloc(self):
        c, nc = self.cfg, self.nc
        KC, TW, TT = c.KC, c.TW, c.TT
        off = 0

        def take(n):
            nonlocal off
            o = off
            off += (n + 63) // 64 * 64
            return o

        m1_need = 2 * c.CC * TT * 2 + 2 * TW * 4 * 2 + 2 * TT * 4 + 3 * TW * 4 + 64 + 2 * c.CPG * TT * 2
        m2_need = 2 * c.CC * TT * 2 + 2 * 4 * TT * 4
        T_size = max(KC * TW * 4, m1_need, m2_need, c.NVB * 128 * 4)
        MG_size = max(KC * TT * 2, 2 * c.XW * 4 + 2 * 4 * TW * 2 + 2 * TW * 4, 4 * TT * 4 + 4 * TT * 2)
        H_size = max(KC * TW * 2, 2 * c.XW * 4)
        T_off = take(T_size)
        H_off = take(H_size)
        MG_off = take(MG_size)
        W_off = take(c.NS * c.SLOT)
        colv_off = take(c.NVB * 128 * 4)
        modc_off = take(6 * KC * 4)
        amod_off = take(2 * KC * 4)
        cs_off = take(KC * 2)
        ones_off = take(128 * 2)
        modloc_off = take(c.MC * 4)
        ident_off = take(128 * 4)
        rstd_off = take(TW * 4)
        htmp2_off = take(2 * TW * 4)
        hm_off = take(c.NT * 4)
        icnt_off = take(c.NT * 4 * 16 * 4)
        total = off
        arena_t = nc.alloc_sbuf_tensor("arena", [128, total], U8)
        self.arena = A = arena_t[:]
        self.total_sbuf = total

        self.T = View(A, T_off, F32, KC, TW)
        self.H = View(A, H_off, BF16, KC, TW)
        self.MG = View(A, MG_off, BF16, KC, TT)
        self.W_off = W_off
        o = T_off
        self.Z = View(A, o, BF16, c.CC, TT); o += c.CC * TT * 2
        self.MX = View(A, o, BF16, c.CC, TT); o += c.CC * TT * 2
        self.gcs = View(A, o, F32, 2, TW); o += 2 * TW * 4
        self.u = View(A, o, F32, 2, TW); o += 2 * TW * 4
        self.yc = View(A, o, F32, 2, TT); o += 2 * TT * 4
        self.pb = View(A, o, F32, 3, TW); o += 3 * TW * 4
        self.t16 = View(A, o, F32, 1, 16); o += 64
        self.pooled = [View(A, o + i * c.CPG * TT * 2, BF16, c.CPG, TT) for i in range(2)]
        o += 2 * c.CPG * TT * 2
        m1_end = o
        o = T_off + 2 * c.CC * TT * 2
        self.sgA = View(A, o, F32, 4, TT); o += 4 * TT * 4
        self.sgB = View(A, o, F32, 4, TT); o += 4 * TT * 4
        assert max(o, m1_end) <= T_off + T_size, (o, m1_end, T_off + T_size)
        self.vstg = View(A, T_off, F32, c.NVB, 128)
        o = MG_off
        self.stgA = [View(A, o + i * c.XW * 4, F32, 1, c.XW) for i in range(2)]
        o += 2 * c.XW * 4
        self.sq4 = [View(A, o + i * 4 * TW * 2, BF16, 4, TW) for i in range(2)]
        o += 2 * 4 * TW * 2
        self.htmp = [View(A, o + i * TW * 4, F32, 1, TW) for i in range(2)]
        self.htmp += [View(A, htmp2_off + i * TW * 4, F32, 1, TW) for i in range(2)]
        o += 2 * TW * 4
        assert o <= MG_off + MG_size, (o - MG_off, MG_size)
        o = MG_off
        self.sgt = View(A, o, F32, 4, TT); o += 4 * TT * 4
        self.actb = View(A, o, BF16, 4, TT); o += 4 * TT * 2
        assert o <= MG_off + MG_size
        self.stgH = [View(A, H_off + i * c.XW * 4, F32, 1, c.XW) for i in range(2)]
        self.colv = View(A, colv_off, F32, 1, c.NVB * 128)
        self.modc = View(A, modc_off, F32, 1, 6 * KC)
        self.amod = View(A, amod_off, F32, 1, 2 * KC)
        self.csb = View(A, cs_off, BF16, 1, KC)
        self.ones = View(A, ones_off, BF16, 1, 128)
        self.modloc = View(A, modloc_off, F32, 1, c.MC)
        self.ident = View(A, ident_off, F32, 1, 128)
        self.rstd = View(A, rstd_off, F32, 1, TW)
        self.hm = View(A, hm_off, F32, 1, c.NT)
        self.icnt = View(A, icnt_off, F32, 1, c.NT * 4 * 16)
        self.ps = [nc.alloc_psum_tensor("ps%d" % b, [128, 512], F32) for b in range(8)]

    def bank(self):
        while True:
            b = self.bank_rr
            self.bank_rr = (self.bank_rr + 1) % 8
            if b not in self.pinned:
                return b

    def pacc(self, b, lo=0, hi=512, p0=0, p1=128):
        return Acc(self.ps[b][p0:p1, lo:hi], [("ps", b)])

    def pacc3(self, b, n, w, lo=0, hi=None, p0=0, p1=128):
        hi = w if hi is None else hi
        ap = self.ps[b][p0:p1, 0:n * w].rearrange("p (a b) -> p a b", b=w)[:, :, lo:hi]
        return Acc(ap, [("ps", b)])

    def colcol(self, idx):
        return self.colv.c(0, idx, idx + 1)

    def load_panel(self, dram2d, r0, nk, c0, ncols):
        c = self.cfg
        assert nk * ncols * 2 <= c.SLOT
        s = self.panel_i % c.NS
        self.panel_i += 1
        v = View(self.arena, self.W_off + s * c.SLOT, BF16, nk, ncols)
        src = dram2d.panel(r0, nk, c0, ncols)
        dst = v.cs(0, nk)
        v2 = self.arena[:, v.off:v.off + nk * ncols * 2].bitcast(BF16)
        self.S.add("pool", lambda e, o=v2, i=src: e.dma_start(out=o, in_=i),
                   writes=[dst], dma="w%d" % s)
        return v

    def mm(self, out, lhsT, rhs, start, stop, **kw):
        self.S.add("pe", lambda e, o=out.ap, l=lhsT.ap, r=rhs.ap: e.matmul(o, l, r, start=start, stop=stop, **kw),
                   reads=[lhsT, rhs], writes=[out])

    def act(self, out, in_, func, bias=None, scale=None, extra_reads=()):
        kw = {}
        rd = [in_] + list(extra_reads)
        if bias is not None:
            if isinstance(bias, Acc):
                kw["bias"] = bias.ap
                rd.append(bias)
            else:
                kw["bias"] = bias
        if scale is not None:
            if isinstance(scale, Acc):
                kw["scale"] = scale.ap
                rd.append(scale)
            else:
                kw["scale"] = scale
        self.S.add("act", lambda e, o=out.ap, i=in_.ap: e.activation(o, i, func, **kw), reads=rd, writes=[out])

    def tt(self, out, a, b, op):
        self.S.add("dve", lambda e, o=out.ap, x=a.ap, y=b.ap: e.tensor_tensor(o, x, y, op),
                   reads=[a, b], writes=[out])

    def stt(self, out, in0, scalar, in1, op0, op1):
        rd = [in0, in1]
        if isinstance(scalar, Acc):
            rd.append(scalar)
            sc = scalar.ap
        else:
            sc = scalar
        self.S.add("dve", lambda e, o=out.ap, x=in0.ap, y=in1.ap: e.scalar_tensor_tensor(o, x, sc, y, op0, op1),
                   reads=rd, writes=[out])

    def ts(self, out, in0, s1, op0):
        rd = [in0]
        if isinstance(s1, Acc):
            rd.append(s1)
            sc = s1.ap
        else:
            sc = s1
        self.S.add("dve", lambda e, o=out.ap, x=in0.ap: e.tensor_scalar(o, x, sc, None, op0), reads=rd, writes=[out])

    def copy(self, out, in_, eng=None):
        if eng is None:
            eng = "act" if self.evac_i % 2 == 0 else "dve"
            self.evac_i += 1
        if eng == "act":
            self.S.add("act", lambda e, o=out.ap, i=in_.ap: e.copy(o, i), reads=[in_], writes=[out])
        else:
            self.S.add("dve", lambda e, o=out.ap, i=in_.ap: e.tensor_copy(o, i), reads=[in_], writes=[out])

    def proj(self, wd, k0, nk_total, col0, nch, src, pieces, kp=None):
        c = self.cfg
        KP = min(c.KP if kp is None else kp, nk_total)
        banks = [[self.bank() for _ in pieces] for _ in range(nch)]
        for half in range(nk_total // KP):
            pan = self.load_panel(wd, k0 + half * KP, KP, col0, nch * 128)
            for ch in range(nch):
                for kc in range(KP):
                    kg = half * KP + kc
                    for pi, (lo, hi) in enumerate(pieces):
                        self.mm(self.pacc(banks[ch][pi], 0, hi - lo), pan.c(kc, ch * 128, (ch + 1) * 128),
                                src.c(kg, lo, hi), kg == 0, kg == nk_total - 1)
            for _ in range(max(1, KP // c.KP)):
                self.mod_tick()
        return banks

    def mod_unit(self, col0, ncols, half):
        c = self.cfg
        pan = self.load_panel(self.w_ada, half * c.KP, c.KP, col0, ncols)
        for j in range(ncols // 128):
            col = col0 // 128 + j
            for kc in range(c.KP):
                kg = half * c.KP + kc
                self.mm(self.pacc(self.mb, col, col + 1), pan.c(kc, j * 128, (j + 1) * 128),
                        self.csb.c(0, kg, kg + 1), False, False, skip_group_check=True)

    def mod_tick(self, flush=False):
        if not self.mod_pending:
            return
        self.mod_acc += self.mod_rate
        while self.mod_pending and (flush or self.mod_acc >= 1.0):
            self.mod_acc -= 1.0
            self.mod_unit(*self.mod_pending.pop(0))

    def mod_finalize(self, m0, m1):
        KC = self.cfg.KC
        self.tt(self.modc.c(0, m0, m1), self.pacc(self.mb, m0, m1), self.colv.c(0, KC + m0, KC + m1), ALU.add)

    def load_x_T(self, t, stg, with_halo):
        c = self.cfg
        blocks = []
        if with_halo:
            blocks.append((t * c.TT, 16, 0))
        for j in range(c.TT // 128):
            blocks.append((t * c.TT + c.HALO + j * 128, 128, c.HALO + j * 128))
        nh = c.D // c.XW
        qn = c.XW // 128
        for (r0, nt, w0) in blocks:
            for hh in range(nh):
                sv = stg[self.stg_i % 2]
                self.stg_i += 1
                d = sv.c(0, 0, c.XW, 0, nt)
                src = self.xs[r0:r0 + nt, hh * c.XW:(hh + 1) * c.XW]
                self.S.add("sp", lambda e, o=d.ap, i=src: e.dma_start(out=o, in_=i), writes=[d],
                           dma="xs%d" % ((self.stg_i - 1) % 2))
                for q4 in range(qn // 4):
                    b = self.bank()
                    for qq in range(4):
                        q = q4 * 4 + qq
                        o = self.pacc(b, qq * 128, qq * 128 + nt)
                        i = sv.c(0, q * 128, (q + 1) * 128, 0, nt)
                        idn = self.ident.c(0, 0, nt, 0, nt)
                        self.S.add("pe", lambda e, o=o.ap, i=i.ap, d=idn.ap: e.transpose(o, i, d),
                                   reads=[i, idn], writes=[o])
                    c0 = hh * qn + q4 * 4
                    self.copy(self.T.cs(c0, c0 + 4, w0, w0 + nt), self.pacc3(b, 4, 128, 0, nt))

    def rms_rstd(self, pieces):
        c = self.cfg
        banks = [self.bank() for _ in pieces]
        for b in banks:
            self.pinned.add(b)
        lo_a = min(p[0] for p in pieces)
        hi_a = max(p[1] for p in pieces)
        for c4 in range(0, c.KC, 4):
            sq = self.sq4[(c4 // 4) % 2]
            self.act(sq.cs(0, 4, lo_a, hi_a), self.T.cs(c4, c4 + 4, lo_a, hi_a), AF.Square)
            for j in range(4):
                ch = c4 + j
                for pi, (lo, hi) in enumerate(pieces):
                    self.mm(self.pacc(banks[pi], 0, hi - lo), self.ones.c(0), sq.c(j, lo, hi), ch == 0, ch == c.KC - 1)
        for pi, (lo, hi) in enumerate(pieces):
            self.act(self.rstd.c(0, lo, hi), self.pacc(banks[pi], 0, hi - lo), AF.Sqrt, bias=self.epsb, scale=1.0 / c.D)
            r = self.rstd.c(0, lo, hi)
            self.S.add("dve", lambda e, o=r.ap: e.reciprocal(o, o), reads=[r], writes=[r])
        for b in banks:
            self.pinned.discard(b)

    def norm_to_h(self, pieces, a0, s0, stats=True):
        c = self.cfg
        if stats:
            self.rms_rstd(pieces)
        lo = min(p[0] for p in pieces)
        hi = max(p[1] for p in pieces)
        for ch in range(c.KC):
            tmp = self.htmp[ch % 4]
            self.tt(tmp.c(0, lo, hi), self.T.c(ch, lo, hi), self.rstd.c(0, lo, hi), ALU.mult)
            self.act(self.H.c(ch, lo, hi), tmp.c(0, lo, hi), AF.Identity,
                     bias=self.modc.c(0, s0 + ch, s0 + ch + 1), scale=self.amod.c(0, a0 + ch, a0 + ch + 1))

    def setup(self):
        c = self.cfg
        KC = c.KC
        S = self.S
        v = self.vstg.cs(0, c.NVB)
        S.add("sp", lambda e, o=v.ap, i=self.vecs.rearrange("(b p) f -> p b f", p=128): e.dma_start(out=o, in_=i),
              writes=[v], dma="c0")
        idn = self.ident.c(0)
        S.add("sp", lambda e, o=idn.ap, i=self.ident_d: e.dma_start(out=o, in_=i), writes=[idn], dma="c1")
        hm = self.hm.c(0)
        S.add("sp", lambda e, o=hm.ap, i=self.hm_d: e.dma_start(out=o, in_=i), writes=[hm], dma="c2")
        ic = self.icnt.c(0)
        S.add("sp", lambda e, o=ic.ap, i=self.icnt_d: e.dma_start(out=o, in_=i), writes=[ic], dma="c3")
        on = self.ones.c(0)
        S.add("dve", lambda e, o=on.ap: e.memset(o, 1.0), writes=[on])
        b = self.bank()
        for blk in range(c.NVB):
            o = self.pacc(b, blk * 128, (blk + 1) * 128)
            i = self.vstg.c(blk)
            S.add("pe", lambda e, o=o.ap, i=i.ap, d=idn.ap: e.transpose(o, i, d), reads=[i, idn], writes=[o])
        self.copy(self.colv.c(0), self.pacc(b, 0, c.NVB * 128), eng="dve")
        self.act(self.csb.c(0), self.colv.c(0, 0, KC), AF.Silu)
        self.epsb = self.amod_eps()
        halves = [(0, c.TW // 2), (c.TW // 2, c.TW)]
        self.load_x_T(0, self.stgA, True)
        self.rms_rstd(halves)
        self.mb = mb = self.bank()
        self.pinned.add(mb)
        mp = self.pacc(mb)
        S.add("dve", lambda e, o=mp.ap: e.memset(o, 0.0), writes=[mp])
        units = []
        col0 = 0
        while col0 < 6 * c.D:
            ncols = min(512, 6 * c.D - col0)
            for half in range(KC // c.KP):
                units.append((col0, ncols, half))
            col0 += ncols
        first = [u for u in units if u[0] < 2 * c.D]
        self.mod_pending = [u for u in units if u[0] >= 2 * c.D]
        for u in first:
            self.mod_unit(*u)
        self.mod_finalize(0, 2 * KC)
        self.stt(self.amod.c(0, 0, KC), self.modc.c(0, KC, 2 * KC), 1.0, self.colv.c(0, 7 * KC, 8 * KC), ALU.add, ALU.mult)
        npan = (c.CC // 2) * 4 * (KC // c.KP) + (c.D // 512) * (2 * (KC // c.KP) + 2 * max(1, c.CC // c.KP))
        self.mod_rate = len(self.mod_pending) / (0.9 * npan)

    def mod_rest(self):
        c = self.cfg
        KC = c.KC
        self.mod_tick(flush=True)
        self.mod_finalize(2 * KC, 6 * KC)
        self.pinned.discard(self.mb)
        self.stt(self.amod.c(0, KC, 2 * KC), self.modc.c(0, 4 * KC, 5 * KC), 1.0, self.colv.c(0, 8 * KC, 9 * KC),
                 ALU.add, ALU.mult)

    def amod_eps(self):
        c = self.cfg
        if c.NVB * 128 > c.NV:
            a = self.colv.c(0, c.NV, c.NV + 1)
        else:
            raise AssertionError("no room for eps column")
        self.S.add("dve", lambda e, o=a.ap: e.memset(o, EPS), reads=[self.colv.c(0, 0, 1)], writes=[a])
        return a

    def tile(self, t):
        c = self.cfg
        S = self.S
        KC, CC, TT, TW, H0 = c.KC, c.CC, c.TT, c.TW, c.HALO
        own = [(H0, TW)]
        halves = [(0, TW // 2), (TW // 2, TW)]
        V0 = KC
        SH1, SC1, G1, SH2, SC2, G2 = 0, KC, 2 * KC, 3 * KC, 4 * KC, 5 * KC
        CV_BGC, CV_BGP, CV_CW, CV_PS, CV_GF = 10 * KC, 11 * KC, 12 * KC, 12 * KC + 3 * CC, 9 * KC

        if t == 0:
            self.norm_to_h(halves, 0, SH1, stats=False)
        else:
            self.load_x_T(t, self.stgA, True)
            self.norm_to_h(halves, 0, SH1)

        hmcol = self.hm.c(0, t, t + 1)
        for cp in range(CC // 2):
            bg = self.proj(self.w_in, 0, KC, c.CW + cp * 256, 2, self.H, halves, kp=KC)
            for ch in range(2):
                for pi, (lo, hi) in enumerate(halves):
                    self.copy(self.gcs.c(ch, lo, hi), self.pacc(bg[ch][pi], 0, hi - lo), eng="act")
            bv = self.proj(self.w_in, 0, KC, 2 * c.CW + cp * 256, 2, self.H, halves, kp=KC)
            for ch in range(2):
                cg_ = cp * 2 + ch
                for pi, (lo, hi) in enumerate(halves):
                    self.tt(self.u.c(ch, lo, hi), self.gcs.c(ch, lo, hi), self.pacc(bv[ch][pi], 0, hi - lo), ALU.mult)
                self.ts(self.u.c(ch, 0, H0), self.u.c(ch, 0, H0), hmcol, ALU.mult)
                w0 = self.colcol(CV_CW + 0 * CC + cg_)
                w1 = self.colcol(CV_CW + 1 * CC + cg_)
                w2 = self.colcol(CV_CW + 2 * CC + cg_)
                self.ts(self.yc.c(ch), self.u.c(ch, H0, TW), w2, ALU.mult)
                self.stt(self.yc.c(ch), self.u.c(ch, H0 - 1, TW - 1), w1, self.yc.c(ch), ALU.mult, ALU.add)
                self.stt(self.yc.c(ch), self.u.c(ch, H0 - 2, TW - 2), w0, self.yc.c(ch), ALU.mult, ALU.add)
            bb = self.proj(self.w_in, 0, KC, 0 + cp * 256, 2, self.H, own, kp=KC)
            for ch in range(2):
                self.tt(self.Z.c(cp * 2 + ch), self.yc.c(ch), self.pacc(bb[ch][0]), ALU.mult)

        for cp in range(CC // 2):
            bp = self.proj(self.w_in, 0, KC, 3 * c.CW + cp * 256, 2, self.H, halves, kp=KC)
            for ch in range(2):
                pc = cp * 2 + ch
                g = pc // c.CPG
                kin = pc % c.CPG
                w = POOL_WINDOWS[g]
                p0 = self.pb
                for pi, (lo, hi) in enumerate(halves):
                    self.copy(p0.c(0, lo, hi), self.pacc(bp[ch][pi], 0, hi - lo), eng="act")
                self.ts(p0.c(0, 0, H0), p0.c(0, 0, H0), hmcol, ALU.mult)
                src = 0
                for i in range(g + 1):
                    sh = 1 << i
                    st = (1 << (i + 1)) - 1
                    dst = 1 + (i % 2)
                    self.tt(self.pb.c(dst, st, TW), self.pb.c(src, st, TW), self.pb.c(src, st - sh, TW - sh), ALU.add)
                    src = dst
                pl = self.pooled[g % 2]
                ic = self.icnt.c(0, (t * 4 + g) * 16, (t * 4 + g) * 16 + 16)
                self.tt(self.t16.c(0), self.pb.c(src, H0, H0 + 16), ic, ALU.mult)
                self.tt(pl.c(kin, 0, 16), self.t16.c(0), p0.c(0, H0, H0 + 16), ALU.subtract)
                self.stt(pl.c(kin, 16, TT), self.pb.c(src, H0 + 16, TW), 1.0 / w, p0.c(0, H0 + 16, TW),
                         ALU.mult, ALU.subtract)
                if kin == c.CPG - 1:
                    pan = self.load_panel(self.w_pg, g * c.CPG, c.CPG, 0, c.PGS)
                    for e_ in range(c.CPG):
                        b = self.bank()
                        for kc in range(c.CPG):
                            self.mm(self.pacc(b), pan.c(kc, e_ * 128, (e_ + 1) * 128), pl.c(kc), kc == 0, kc == c.CPG - 1)
                        mc = g * c.CPG + e_
                        self.act(self.MX.c(mc), self.pacc(b), AF.Copy, scale=self.colcol(CV_PS + mc))

        for cg in range(c.D // 512):
            bgc = self.proj(self.w_in, 0, KC, 4 * c.CW + cg * 512, 4, self.H, own)
            for j in range(4):
                self.act(self.sgA.c(j), self.pacc(bgc[j][0]), AF.Sigmoid, bias=self.colcol(CV_BGC + cg * 4 + j))
            byc = self.proj(self.w_conv_out, 0, CC, cg * 512, 4, self.Z, [(0, TT)])
            for j in range(4):
                self.tt(self.sgA.c(j), self.sgA.c(j), self.pacc(byc[j][0]), ALU.mult)
            bgp = self.proj(self.w_in, 0, KC, 4 * c.CW + c.D + cg * 512, 4, self.H, own)
            for j in range(4):
                self.act(self.sgB.c(j), self.pacc(bgp[j][0]), AF.Sigmoid, bias=self.colcol(CV_BGP + cg * 4 + j))
            byp = self.proj(self.w_pool_out, 0, CC, cg * 512, 4, self.MX, [(0, TT)])
            for j in range(4):
                self.tt(self.sgB.c(j), self.sgB.c(j), self.pacc(byp[j][0]), ALU.mult)
                self.tt(self.MG.c(cg * 4 + j), self.sgA.c(j), self.sgB.c(j), ALU.add)

        if t == 0:
            self.mod_rest()
        self.load_x_T(t, self.stgH, False)
        for cg in range(c.D // 512):
            bo = self.proj(self.w_o, 0, KC, cg * 512, 4, self.MG, [(0, TT)])
            for j in range(4):
                n = cg * 4 + j
                self.stt(self.T.c(n, H0, TW), self.pacc(bo[j][0]), self.modc.c(0, G1 + n, G1 + n + 1),
                         self.T.c(n, H0, TW), ALU.mult, ALU.add)

        self.norm_to_h(own, KC, SH2)

        nfg = (c.FC + 3) // 4
        for fg in range(nfg):
            ncf = min(4, c.FC - 4 * fg)
            bgt = self.proj(self.w_gu, 0, KC, fg * 512, ncf, self.H, own)
            for j in range(ncf):
                self.act(self.sgt.c(j), self.pacc(bgt[j][0]), AF.Silu)
            bup = self.proj(self.w_gu, 0, KC, c.F + fg * 512, ncf, self.H, own)
            for j in range(ncf):
                self.tt(self.actb.c(j), self.sgt.c(j), self.pacc(bup[j][0]), ALU.mult)
            for hc in range(c.D // c.DW):
                pan = self.load_panel(self.w_down, fg * 4, ncf, hc * c.DW, c.DW)
                for j in range(c.DW // 128):
                    n = hc * (c.DW // 128) + j
                    b = self.bank()
                    for fc in range(ncf):
                        self.mm(self.pacc(b), pan.c(fc, j * 128, (j + 1) * 128), self.actb.c(fc), fc == 0, fc == ncf - 1)
                    self.stt(self.T.c(n, H0, TW), self.pacc(b), self.modc.c(0, G2 + n, G2 + n + 1),
                             self.T.c(n, H0, TW), ALU.mult, ALU.add)

        self.rms_rstd(own)
        for ch in range(KC):
            self.stt(self.T.c(ch, H0, TW), self.T.c(ch, H0, TW), self.colcol(CV_GF + ch), self.rstd.c(0, H0, TW),
                     ALU.mult, ALU.mult)
        qn = c.XW // 128
        idn = self.ident.c(0)
        for jb in range(TT // 128):
            w0 = H0 + jb * 128
            for hh in range(c.D // c.XW):
                sv = self.stgH[self.stg_i % 2]
                si = self.stg_i % 2
                self.stg_i += 1
                for q4 in range(qn // 4):
                    b = self.bank()
                    for qq in range(4):
                        ch = hh * qn + q4 * 4 + qq
                        o = self.pacc(b, qq * 128, (qq + 1) * 128)
                        i = self.T.c(ch, w0, w0 + 128)
                        S.add("pe", lambda e, o=o.ap, i=i.ap, d=idn.ap: e.transpose(o, i, d), reads=[i, idn], writes=[o])
                    self.copy(sv.c(0, q4 * 512, (q4 + 1) * 512), self.pacc(b))
                s_ = sv.c(0)
                r0 = t * TT + jb * 128
                dst = self.y[r0:r0 + 128, hh * c.XW:(hh + 1) * c.XW]
                S.add("sp", lambda e, o=dst, i=s_.ap: e.dma_start(out=o, in_=i), reads=[s_],
                      writes=[[("y", t, jb, hh)]], dma="st%d" % si)

    def finish(self):
        c = self.cfg
        allw = [[("y", t, jb, hh)] for t in range(c.NT) for jb in range(c.TT // 128) for hh in range(c.D // c.XW)]
        self.S.add("sp", lambda e: e.nop(), reads=allw)

    def build(self):
        self.declare()
        self.alloc()
        self.setup()
        for t in range(self.cfg.NT):
            self.tile(t)
        self.finish()
        nc = self.nc
        S = self.S
        dkeys = S.finalize()
        with contextlib.ExitStack() as es:
            sems = {}
            for k in list(ENGS) + dkeys:
                sems[k] = es.enter_context(nc.semaphore("s_" + k))
            block = es.enter_context(nc.Block())

            @block.tensor
            def _(e):
                S.emit("pe", e, sems)

            @block.scalar
            def _(e):
                S.emit("act", e, sems)

            @block.vector
            def _(e):
                S.emit("dve", e, sems)

            @block.gpsimd
            def _(e):
                S.emit("pool", e, sems)

            @block.sync
            def _(e):
                S.emit("sp", e, sems)
        return nc


def host_inputs(cfg, specs, x, c, w_ada, b_ada, norm1_gain, w_in, b_branch_gate, conv_w, w_conv_out, w_pool_group,
                pool_scale, w_pool_out, w_o, norm2_gain, w_gate_up, w_down, final_norm_gain):
    f = lambda a: np.ascontiguousarray(np.asarray(a, dtype=np.float32))
    D = cfg.D
    x2 = f(x).reshape(cfg.S, D)
    rows = [f(c).reshape(-1), f(b_ada).reshape(-1), f(norm1_gain).reshape(-1), f(norm2_gain).reshape(-1),
            f(final_norm_gain).reshape(-1), f(b_branch_gate).reshape(-1), f(conv_w).reshape(-1), f(pool_scale).reshape(-1)]
    vec = np.concatenate(rows)
    assert vec.size == cfg.NV * 128
    vecs = np.zeros((cfg.NVB * 128, 128), np.float32)
    vecs.reshape(-1)[:vec.size] = vec
    ident = np.eye(128, dtype=np.float32)
    mats = {
        "w_ada": f(w_ada).reshape(D, 6 * D), "w_in": f(w_in).reshape(D, 4 * D),
        "w_conv_out": f(w_conv_out).reshape(cfg.CW, D), "w_pool_group": f(w_pool_group).reshape(4 * cfg.PGS, cfg.PGS),
        "w_pool_out": f(w_pool_out).reshape(cfg.CW, D), "w_o": f(w_o).reshape(D, D),
        "w_gate_up": f(w_gate_up).reshape(D, 2 * cfg.F), "w_down": f(w_down).reshape(cfg.F, D),
    }
    shared = {"vecs": vecs, "ident": ident}
    for name, order in specs.items():
        W = mats[name]
        flat = np.empty(W.size, np.float32)
        o = 0
        for (r0, nk, c0, ncols) in order:
            blk = W[r0 * 128:(r0 + nk) * 128, c0:c0 + ncols].reshape(nk, 128, ncols)
            n = 128 * nk * ncols
            flat[o:o + n].reshape(128, nk, ncols)[...] = blk.transpose(1, 0, 2)
            o += n
        assert o == W.size, (name, o, W.size)
        shared[name] = flat
    xpad = np.concatenate([np.zeros((cfg.HALO, D), np.float32), x2], axis=0)
    in_maps = []
    for core in range(cfg.NCORES):
        m = dict(shared)
        m["xs"] = np.ascontiguousarray(xpad[core * cfg.TPC: (core + 1) * cfg.TPC + cfg.HALO])
        hm = np.ones((128, cfg.NT), np.float32)
        icnt = np.zeros((cfg.NT, 4, 16), np.float32)
        for t in range(cfg.NT):
            g0 = core * cfg.TPC + t * cfg.TT
            if g0 == 0:
                hm[:, t] = 0.0
            for g, w in enumerate(POOL_WINDOWS):
                pos = g0 + np.arange(16)
                icnt[t, g] = 1.0 / np.minimum(pos + 1, w)
        m["hm"] = hm
        m["icnt"] = np.ascontiguousarray(np.broadcast_to(icnt.reshape(1, -1), (128, cfg.NT * 64)))
        in_maps.append(m)
    return in_maps


_NC_CACHE = {}


def run(cfg, inputs):
    key = (cfg.D, cfg.F, cfg.S, cfg.KP, cfg.NS)
    if key not in _NC_CACHE:
        b = Builder(cfg)
        nc = b.build()
        _NC_CACHE[key] = (nc, {w.name: list(w.order) for w in b.wrefs})
    nc, specs = _NC_CACHE[key]
    in_maps = host_inputs(cfg, specs, **inputs)
    res = run_bass_kernel_spmd(nc, in_maps, core_ids=list(range(cfg.NCORES)))
    out = np.concatenate([np.asarray(r["y"]) for r in res.results], axis=0)
    return out.reshape(1, cfg.S, cfg.D).astype(np.float32)


def kernel(**inputs):
    cfg = Cfg()
    return run(cfg, inputs)
```

```python
import contextlib
import numpy as np
import concourse.bass as bass
import concourse.mybir as mybir
from concourse.bass_utils import run_bass_kernel_spmd

F32 = mybir.dt.float32
BF16 = mybir.dt.bfloat16
U8 = mybir.dt.uint8
AF = mybir.ActivationFunctionType
ALU = mybir.AluOpType

POOL_WINDOWS = (2, 4, 8, 16)
EPS = 1e-6
GRAN = 16
SAME_ENG_SYNC = True


class Cfg:
    def __init__(self, D=4096, F=11008, S=8192, ncores=8, KP=16, NS=4):
        self.D, self.F, self.S, self.NCORES, self.KP, self.NS = D, F, S, ncores, KP, NS
        self.KC = D // 128
        self.CW = D // 2
        self.CC = self.CW // 128
        self.PGS = self.CW // 4
        self.CPG = self.PGS // 128
        self.FC = F // 128
        self.TPC = S // ncores
        self.TT = 512
        self.NT = self.TPC // self.TT
        self.HALO = 16
        self.TW = self.TT + self.HALO
        self.XW = min(D, 2048)
        self.DW = min(D, 2048)
        self.NV = 14 * self.KC
        self.NVB = (self.NV + 127) // 128
        self.SLOT = 16384
        self.MC = 6 * self.KC
        assert self.KC % KP == 0 and (self.CC % KP == 0 or self.CC <= KP)


class Acc:
    __slots__ = ("ap", "res")

    def __init__(self, ap, res):
        self.ap = ap
        self.res = res


class Op:
    __slots__ = ("eng", "fn", "deps", "ev", "signal", "dma", "seq")

    def __init__(self, eng, fn, dma, seq):
        self.eng, self.fn, self.dma, self.seq = eng, fn, dma, seq
        self.deps = {}
        self.ev = None
        self.signal = False


ENGS = ("pe", "act", "dve", "pool", "sp")


class Sched:
    def __init__(self):
        self.q = {e: [] for e in ENGS}
        self.res = {}
        self.nops = 0

    def add(self, eng, fn, reads=(), writes=(), dma=None):
        self.nops += 1
        op = Op(eng, fn, dma, self.nops)
        deps = {}
        rkeys = set()
        for a in reads:
            rkeys.update(a.res if isinstance(a, Acc) else a)
        wkeys = set()
        for a in writes:
            wkeys.update(a.res if isinstance(a, Acc) else a)
        res = self.res
        for k in rkeys:
            st = res.get(k)
            if st is not None and st[0] is not None:
                deps[id(st[0])] = st[0]
        for k in wkeys:
            st = res.get(k)
            if st is not None:
                if st[0] is not None:
                    deps[id(st[0])] = st[0]
                for o in st[1].values():
                    deps[id(o)] = o
                for o in st[2]:
                    deps[id(o)] = o
        for k in rkeys:
            if k in wkeys:
                continue
            st = res.get(k)
            if st is None:
                st = res[k] = [None, {}, []]
            if dma is not None:
                st[2].append(op)
            else:
                st[1][eng] = op
        for k in wkeys:
            res[k] = [op, {}, []]
        deps.pop(id(op), None)
        latest = {}
        for d in deps.values():
            if d.dma is not None:
                op.deps[id(d)] = d
            elif d.eng == eng and not (SAME_ENG_SYNC and eng in ("act", "dve")):
                continue
            else:
                o = latest.get(d.eng)
                if o is None or o.seq < d.seq:
                    latest[d.eng] = d
        for d in latest.values():
            op.deps[id(d)] = d
        self.q[eng].append(op)
        return op

    def finalize(self):
        for e in ENGS:
            for op in self.q[e]:
                for d in op.deps.values():
                    d.signal = True
        ecount = {e: 0 for e in ENGS}
        dcount = {}
        for e in ENGS:
            for op in self.q[e]:
                if op.dma is not None:
                    dcount[op.dma] = dcount.get(op.dma, 0) + 16
                    op.ev = (op.dma, dcount[op.dma])
                    op.signal = True
                elif op.signal:
                    ecount[e] += 1
                    op.ev = (e, ecount[e])
        return sorted(dcount.keys())

    def emit(self, eng, h, sems):
        waited = {}
        for op in self.q[eng]:
            need = {}
            for d in op.deps.values():
                if not d.signal:
                    continue
                k, v = d.ev
                if need.get(k, 0) < v:
                    need[k] = v
            for k, v in need.items():
                if waited.get(k, 0) < v:
                    h.wait_ge(sems[k], v)
                    waited[k] = v
            ins = op.fn(h)
            if op.signal:
                k, v = op.ev
                ins.then_inc(sems[k], 16 if op.dma is not None else 1)


class WRef:
    def __init__(self, name, flat_ap, numel):
        self.name, self.flat, self.numel = name, flat_ap, numel
        self.offs = {}
        self.order = []
        self.next = 0

    def panel(self, r0, nk, c0, ncols):
        key = (r0, nk, c0, ncols)
        if key not in self.offs:
            self.offs[key] = self.next
            self.order.append(key)
            self.next += 128 * nk * ncols
            assert self.next <= self.numel, (self.name, key, self.next, self.numel)
        o = self.offs[key]
        return self.flat[o:o + 128 * nk * ncols].rearrange("(p e) -> p e", p=128)


class View:
    def __init__(self, arena_ap, off, dtype, n0, n1):
        self.esz = 4 if dtype == F32 else 2
        self.off, self.n0, self.n1 = off, n0, n1
        self.nbytes = n0 * n1 * self.esz
        self.ap = arena_ap[:, off:off + self.nbytes].bitcast(dtype).rearrange("p (a b) -> p a b", b=n1)

    def _res(self, c0, c1, lo, hi):
        b0 = self.off + (c0 * self.n1 + lo) * self.esz
        b1 = self.off + ((c1 - 1) * self.n1 + hi) * self.esz
        return [("sb", g) for g in range(b0 // GRAN, (b1 - 1) // GRAN + 1)]

    def c(self, c, lo=0, hi=None, p0=0, p1=128):
        hi = self.n1 if hi is None else hi
        return Acc(self.ap[p0:p1, c, lo:hi], self._res(c, c + 1, lo, hi))

    def cs(self, c0, c1, lo=0, hi=None, p0=0, p1=128):
        hi = self.n1 if hi is None else hi
        if lo == 0 and hi == self.n1:
            res = self._res(c0, c1, lo, hi)
        else:
            res = []
            for cc in range(c0, c1):
                res.extend(self._res(cc, cc + 1, lo, hi))
        return Acc(self.ap[p0:p1, c0:c1, lo:hi], res)


class Builder:
    def __init__(self, cfg):
        self.cfg = cfg
        self.S = Sched()
        self.nc = bass.Bass("TRN2", target_bir_lowering=False)
        self.bank_rr = 0
        self.pinned = set()
        self.panel_i = 0
        self.evac_i = 0
        self.stg_i = 0
        self.mod_pending = []
        self.mod_rate = 0.0
        self.mod_acc = 0.0

    def declare(self):
        c, nc = self.cfg, self.nc
        D, F = c.D, c.F

        def inp(name, shape):
            return nc.dram_tensor(name, list(shape), F32, kind="ExternalInput").ap()

        self.xs = inp("xs", [c.TPC + c.HALO, D])
        self.vecs = inp("vecs", [c.NVB * 128, 128])
        self.ident_d = inp("ident", [128, 128])
        self.hm_d = inp("hm", [128, c.NT])
        self.icnt_d = inp("icnt", [128, c.NT * 4 * 16])
        def winp(name, numel):
            return WRef(name, inp(name, [numel]), numel)

        self.w_ada = winp("w_ada", D * 6 * D)
        self.w_in = winp("w_in", D * 4 * D)
        self.w_conv_out = winp("w_conv_out", c.CW * D)
        self.w_pg = winp("w_pool_group", 4 * c.PGS * c.PGS)
        self.w_pool_out = winp("w_pool_out", c.CW * D)
        self.w_o = winp("w_o", D * D)
        self.w_gu = winp("w_gate_up", D * 2 * F)
        self.w_down = winp("w_down", F * D)
        self.wrefs = [self.w_ada, self.w_in, self.w_conv_out, self.w_pg, self.w_pool_out, self.w_o, self.w_gu,
                      self.w_down]
        self.y = nc.dram_tensor("y", [c.TPC, D], F32, kind="ExternalOutput").ap()

    def alloc(self):
        c, nc = self.cfg, self.nc
        KC, TW, TT = c.KC, c.TW, c.TT
        off = 0

        def take(n):
            nonlocal off
            o = off
            off += (n + 63) // 64 * 64
            return o

        m1_need = 2 * c.CC * TT * 2 + 2 * TW * 4 * 2 + 2 * TT * 4 + 3 * TW * 4 + 64 + 2 * c.CPG * TT * 2
        m2_need = 2 * c.CC * TT * 2 + 2 * 4 * TT * 4
        T_size = max(KC * TW * 4, m1_need, m2_need, c.NVB * 128 * 4)
        MG_size = max(KC * TT * 2, 2 * c.XW * 4 + 2 * 4 * TW * 2 + 2 * TW * 4, 4 * TT * 4 + 4 * TT * 2)
        H_size = max(KC * TW * 2, 2 * c.XW * 4)
        T_off = take(T_size)
        H_off = take(H_size)
        MG_off = take(MG_size)
        W_off = take(c.NS * c.SLOT)
        colv_off = take(c.NVB * 128 * 4)
        modc_off = take(6 * KC * 4)
        amod_off = take(2 * KC * 4)
        cs_off = take(KC * 2)
        ones_off = take(128 * 2)
        modloc_off = take(c.MC * 4)
        ident_off = take(128 * 4)
        rstd_off = take(TW * 4)
        htmp2_off = take(2 * TW * 4)
        hm_off = take(c.NT * 4)
        icnt_off = take(c.NT * 4 * 16 * 4)
        total = off
        arena_t = nc.alloc_sbuf_tensor("arena", [128, total], U8)
        self.arena = A = arena_t[:]
        self.total_sbuf = total

        self.T = View(A, T_off, F32, KC, TW)
        self.H = View(A, H_off, BF16, KC, TW)
        self.MG = View(A, MG_off, BF16, KC, TT)
        self.W_off = W_off
        o = T_off
        self.Z = View(A, o, BF16, c.CC, TT); o += c.CC * TT * 2
        self.MX = View(A, o, BF16, c.CC, TT); o += c.CC * TT * 2
        self.gcs = View(A, o, F32, 2, TW); o += 2 * TW * 4
        self.u = View(A, o, F32, 2, TW); o += 2 * TW * 4
        self.yc = View(A, o, F32, 2, TT); o += 2 * TT * 4
        self.pb = View(A, o, F32, 3, TW); o += 3 * TW * 4
        self.t16 = View(A, o, F32, 1, 16); o += 64
        self.pooled = [View(A, o + i * c.CPG * TT * 2, BF16, c.CPG, TT) for i in range(2)]
        o += 2 * c.CPG * TT * 2
        m1_end = o
        o = T_off + 2 * c.CC * TT * 2
        self.sgA = View(A, o, F32, 4, TT); o += 4 * TT * 4
        self.sgB = View(A, o, F32, 4, TT); o += 4 * TT * 4
        assert max(o, m1_end) <= T_off + T_size, (o, m1_end, T_off + T_size)
        self.vstg = View(A, T_off, F32, c.NVB, 128)
        o = MG_off
        self.stgA = [View(A, o + i * c.XW * 4, F32, 1, c.XW) for i in range(2)]
        o += 2 * c.XW * 4
        self.sq4 = [View(A, o + i * 4 * TW * 2, BF16, 4, TW) for i in range(2)]
        o += 2 * 4 * TW * 2
        self.htmp = [View(A, o + i * TW * 4, F32, 1, TW) for i in range(2)]
        self.htmp += [View(A, htmp2_off + i * TW * 4, F32, 1, TW) for i in range(2)]
        o += 2 * TW * 4
        assert o <= MG_off + MG_size, (o - MG_off, MG_size)
        o = MG_off
        self.sgt = View(A, o, F32, 4, TT); o += 4 * TT * 4
        self.actb = View(A, o, BF16, 4, TT); o += 4 * TT * 2
        assert o <= MG_off + MG_size
        self.stgH = [View(A, H_off + i * c.XW * 4, F32, 1, c.XW) for i in range(2)]
        self.colv = View(A, colv_off, F32, 1, c.NVB * 128)
        self.modc = View(A, modc_off, F32, 1, 6 * KC)
        self.amod = View(A, amod_off, F32, 1, 2 * KC)
        self.csb = View(A, cs_off, BF16, 1, KC)
        self.ones = View(A, ones_off, BF16, 1, 128)
        self.modloc = View(A, modloc_off, F32, 1, c.MC)
        self.ident = View(A, ident_off, F32, 1, 128)
        self.rstd = View(A, rstd_off, F32, 1, TW)
        self.hm = View(A, hm_off, F32, 1, c.NT)
        self.icnt = View(A, icnt_off, F32, 1, c.NT * 4 * 16)
        self.ps = [nc.alloc_psum_tensor("ps%d" % b, [128, 512], F32) for b in range(8)]

    def bank(self):
        while True:
            b = self.bank_rr
            self.bank_rr = (self.bank_rr + 1) % 8
            if b not in self.pinned:
                return b

    def pacc(self, b, lo=0, hi=512, p0=0, p1=128):
        return Acc(self.ps[b][p0:p1, lo:hi], [("ps", b)])

    def pacc3(self, b, n, w, lo=0, hi=None, p0=0, p1=128):
        hi = w if hi is None else hi
        ap = self.ps[b][p0:p1, 0:n * w].rearrange("p (a b) -> p a b", b=w)[:, :, lo:hi]
        return Acc(ap, [("ps", b)])

    def colcol(self, idx):
        return self.colv.c(0, idx, idx + 1)

    def load_panel(self, dram2d, r0, nk, c0, ncols):
        c = self.cfg
        assert nk * ncols * 2 <= c.SLOT
        s = self.panel_i % c.NS
        self.panel_i += 1
        v = View(self.arena, self.W_off + s * c.SLOT, BF16, nk, ncols)
        src = dram2d.panel(r0, nk, c0, ncols)
        dst = v.cs(0, nk)
        v2 = self.arena[:, v.off:v.off + nk * ncols * 2].bitcast(BF16)
        self.S.add("pool", lambda e, o=v2, i=src: e.dma_start(out=o, in_=i),
                   writes=[dst], dma="w%d" % s)
        return v

    def mm(self, out, lhsT, rhs, start, stop, **kw):
        self.S.add("pe", lambda e, o=out.ap, l=lhsT.ap, r=rhs.ap: e.matmul(o, l, r, start=start, stop=stop, **kw),
                   reads=[lhsT, rhs], writes=[out])

    def act(self, out, in_, func, bias=None, scale=None, extra_reads=()):
        kw = {}
        rd = [in_] + list(extra_reads)
        if bias is not None:
            if isinstance(bias, Acc):
                kw["bias"] = bias.ap
                rd.append(bias)
            else:
                kw["bias"] = bias
        if scale is not None:
            if isinstance(scale, Acc):
                kw["scale"] = scale.ap
                rd.append(scale)
            else:
                kw["scale"] = scale
        self.S.add("act", lambda e, o=out.ap, i=in_.ap: e.activation(o, i, func, **kw), reads=rd, writes=[out])

    def tt(self, out, a, b, op):
        self.S.add("dve", lambda e, o=out.ap, x=a.ap, y=b.ap: e.tensor_tensor(o, x, y, op),
                   reads=[a, b], writes=[out])

    def stt(self, out, in0, scalar, in1, op0, op1):
        rd = [in0, in1]
        if isinstance(scalar, Acc):
            rd.append(scalar)
            sc = scalar.ap
        else:
            sc = scalar
        self.S.add("dve", lambda e, o=out.ap, x=in0.ap, y=in1.ap: e.scalar_tensor_tensor(o, x, sc, y, op0, op1),
                   reads=rd, writes=[out])

    def ts(self, out, in0, s1, op0):
        rd = [in0]
        if isinstance(s1, Acc):
            rd.append(s1)
            sc = s1.ap
        else:
            sc = s1
        self.S.add("dve", lambda e, o=out.ap, x=in0.ap: e.tensor_scalar(o, x, sc, None, op0), reads=rd, writes=[out])

    def copy(self, out, in_, eng=None):
        if eng is None:
            eng = "act" if self.evac_i % 2 == 0 else "dve"
            self.evac_i += 1
        if eng == "act":
            self.S.add("act", lambda e, o=out.ap, i=in_.ap: e.copy(o, i), reads=[in_], writes=[out])
        else:
            self.S.add("dve", lambda e, o=out.ap, i=in_.ap: e.tensor_copy(o, i), reads=[in_], writes=[out])

    def proj(self, wd, k0, nk_total, col0, nch, src, pieces, kp=None):
        c = self.cfg
        KP = min(c.KP if kp is None else kp, nk_total)
        banks = [[self.bank() for _ in pieces] for _ in range(nch)]
        for half in range(nk_total // KP):
            pan = self.load_panel(wd, k0 + half * KP, KP, col0, nch * 128)
            for ch in range(nch):
                for kc in range(KP):
                    kg = half * KP + kc
                    for pi, (lo, hi) in enumerate(pieces):
                        self.mm(self.pacc(banks[ch][pi], 0, hi - lo), pan.c(kc, ch * 128, (ch + 1) * 128),
                                src.c(kg, lo, hi), kg == 0, kg == nk_total - 1)
            for _ in range(max(1, KP // c.KP)):
                self.mod_tick()
        return banks

    def mod_unit(self, col0, ncols):
        c = self.cfg
        KC = c.KC
        pan = self.load_panel(self.w_ada, 0, KC, col0, ncols)
        b = self.bank()
        nj = ncols // 128
        m0 = col0 // 128
        for j in range(nj):
            for kg in range(KC):
                self.mm(self.pacc(b, j, j + 1), pan.c(kg, j * 128, (j + 1) * 128),
                        self.csb.c(0, kg, kg + 1), kg == 0, kg == KC - 1)
        self.tt(self.modc.c(0, m0, m0 + nj), self.pacc(b, 0, nj), self.colv.c(0, KC + m0, KC + m0 + nj), ALU.add)

    def mod_tick(self, flush=False):
        if not self.mod_pending:
            return
        self.mod_acc += self.mod_rate
        while self.mod_pending and (flush or self.mod_acc >= 1.0):
            self.mod_acc -= 1.0
            self.mod_unit(*self.mod_pending.pop(0))

    def load_x_T(self, t, stg, with_halo):
        c = self.cfg
        blocks = []
        if with_halo:
            blocks.append((t * c.TT, 16, 0))
        for j in range(c.TT // 128):
            blocks.append((t * c.TT + c.HALO + j * 128, 128, c.HALO + j * 128))
        nh = c.D // c.XW
        qn = c.XW // 128
        for (r0, nt, w0) in blocks:
            for hh in range(nh):
                sv = stg[self.stg_i % 2]
                self.stg_i += 1
                d = sv.c(0, 0, c.XW, 0, nt)
                src = self.xs[r0:r0 + nt, hh * c.XW:(hh + 1) * c.XW]
                self.S.add("sp", lambda e, o=d.ap, i=src: e.dma_start(out=o, in_=i), writes=[d],
                           dma="xs%d" % ((self.stg_i - 1) % 2))
                for q4 in range(qn // 4):
                    b = self.bank()
                    for qq in range(4):
                        q = q4 * 4 + qq
                        o = self.pacc(b, qq * 128, qq * 128 + nt)
                        i = sv.c(0, q * 128, (q + 1) * 128, 0, nt)
                        idn = self.ident.c(0, 0, nt, 0, nt)
                        self.S.add("pe", lambda e, o=o.ap, i=i.ap, d=idn.ap: e.transpose(o, i, d),
                                   reads=[i, idn], writes=[o])
                    c0 = hh * qn + q4 * 4
                    self.copy(self.T.cs(c0, c0 + 4, w0, w0 + nt), self.pacc3(b, 4, 128, 0, nt))

    def rms_rstd(self, pieces):
        c = self.cfg
        banks = [self.bank() for _ in pieces]
        for b in banks:
            self.pinned.add(b)
        lo_a = min(p[0] for p in pieces)
        hi_a = max(p[1] for p in pieces)
        for c4 in range(0, c.KC, 4):
            sq = self.sq4[(c4 // 4) % 2]
            self.act(sq.cs(0, 4, lo_a, hi_a), self.T.cs(c4, c4 + 4, lo_a, hi_a), AF.Square)
            for j in range(4):
                ch = c4 + j
                for pi, (lo, hi) in enumerate(pieces):
                    self.mm(self.pacc(banks[pi], 0, hi - lo), self.ones.c(0), sq.c(j, lo, hi), ch == 0, ch == c.KC - 1)
        for pi, (lo, hi) in enumerate(pieces):
            self.act(self.rstd.c(0, lo, hi), self.pacc(banks[pi], 0, hi - lo), AF.Sqrt, bias=self.epsb, scale=1.0 / c.D)
            r = self.rstd.c(0, lo, hi)
            self.S.add("dve", lambda e, o=r.ap: e.reciprocal(o, o), reads=[r], writes=[r])
        for b in banks:
            self.pinned.discard(b)

    def norm_to_h(self, pieces, a0, s0, stats=True):
        c = self.cfg
        if stats:
            self.rms_rstd(pieces)
        lo = min(p[0] for p in pieces)
        hi = max(p[1] for p in pieces)
        for ch in range(c.KC):
            tmp = self.htmp[ch % 4]
            self.tt(tmp.c(0, lo, hi), self.T.c(ch, lo, hi), self.rstd.c(0, lo, hi), ALU.mult)
            self.act(self.H.c(ch, lo, hi), tmp.c(0, lo, hi), AF.Identity,
                     bias=self.modc.c(0, s0 + ch, s0 + ch + 1), scale=self.amod.c(0, a0 + ch, a0 + ch + 1))

    def setup(self):
        c = self.cfg
        KC = c.KC
        S = self.S
        v = self.vstg.cs(0, c.NVB)
        S.add("sp", lambda e, o=v.ap, i=self.vecs.rearrange("(b p) f -> p b f", p=128): e.dma_start(out=o, in_=i),
              writes=[v], dma="c0")
        idn = self.ident.c(0)
        S.add("sp", lambda e, o=idn.ap, i=self.ident_d: e.dma_start(out=o, in_=i), writes=[idn], dma="c1")
        hm = self.hm.c(0)
        S.add("sp", lambda e, o=hm.ap, i=self.hm_d: e.dma_start(out=o, in_=i), writes=[hm], dma="c2")
        ic = self.icnt.c(0)
        S.add("sp", lambda e, o=ic.ap, i=self.icnt_d: e.dma_start(out=o, in_=i), writes=[ic], dma="c3")
        on = self.ones.c(0)
        S.add("dve", lambda e, o=on.ap: e.memset(o, 1.0), writes=[on])
        b = self.bank()
        for blk in range(c.NVB):
            o = self.pacc(b, blk * 128, (blk + 1) * 128)
            i = self.vstg.c(blk)
            S.add("pe", lambda e, o=o.ap, i=i.ap, d=idn.ap: e.transpose(o, i, d), reads=[i, idn], writes=[o])
        self.copy(self.colv.c(0), self.pacc(b, 0, c.NVB * 128), eng="dve")
        self.act(self.csb.c(0), self.colv.c(0, 0, KC), AF.Silu)
        self.epsb = self.amod_eps()
        halves = [(0, c.TW // 2), (c.TW // 2, c.TW)]
        self.load_x_T(0, self.stgA, True)
        self.rms_rstd(halves)
        units = [(col0, 256) for col0 in range(0, 6 * c.D, 256)]
        first = [u for u in units if u[0] < 2 * c.D]
        self.mod_pending = [u for u in units if u[0] >= 2 * c.D]
        for u in first:
            self.mod_unit(*u)
        self.stt(self.amod.c(0, 0, KC), self.modc.c(0, KC, 2 * KC), 1.0, self.colv.c(0, 7 * KC, 8 * KC), ALU.add, ALU.mult)
        npan = (c.CC // 2) * 4 * (KC // c.KP) + (c.D // 512) * (2 * (KC // c.KP) + 2 * max(1, c.CC // c.KP))
        self.mod_rate = len(self.mod_pending) / (0.9 * npan)

    def mod_rest(self):
        c = self.cfg
        KC = c.KC
        self.mod_tick(flush=True)
        self.stt(self.amod.c(0, KC, 2 * KC), self.modc.c(0, 4 * KC, 5 * KC), 1.0, self.colv.c(0, 8 * KC, 9 * KC),
                 ALU.add, ALU.mult)

    def amod_eps(self):
        c = self.cfg
        if c.NVB * 128 > c.NV:
            a = self.colv.c(0, c.NV, c.NV + 1)
        else:
            raise AssertionError("no room for eps column")
        self.S.add("dve", lambda e, o=a.ap: e.memset(o, EPS), reads=[self.colv.c(0, 0, 1)], writes=[a])
        return a

    def tile(self, t):
        c = self.cfg
        S = self.S
        KC, CC, TT, TW, H0 = c.KC, c.CC, c.TT, c.TW, c.HALO
        own = [(H0, TW)]
        halves = [(0, TW // 2), (TW // 2, TW)]
        V0 = KC
        SH1, SC1, G1, SH2, SC2, G2 = 0, KC, 2 * KC, 3 * KC, 4 * KC, 5 * KC
        CV_BGC, CV_BGP, CV_CW, CV_PS, CV_GF = 10 * KC, 11 * KC, 12 * KC, 12 * KC + 3 * CC, 9 * KC

        if t == 0:
            self.norm_to_h(halves, 0, SH1, stats=False)
        else:
            self.load_x_T(t, self.stgA, True)
            self.norm_to_h(halves, 0, SH1)

        hmcol = self.hm.c(0, t, t + 1)
        for cp in range(CC // 2):
            bg = self.proj(self.w_in, 0, KC, c.CW + cp * 256, 2, self.H, halves, kp=KC)
            for ch in range(2):
                for pi, (lo, hi) in enumerate(halves):
                    self.copy(self.gcs.c(ch, lo, hi), self.pacc(bg[ch][pi], 0, hi - lo), eng="act")
            bv = self.proj(self.w_in, 0, KC, 2 * c.CW + cp * 256, 2, self.H, halves, kp=KC)
            for ch in range(2):
                cg_ = cp * 2 + ch
                for pi, (lo, hi) in enumerate(halves):
                    self.tt(self.u.c(ch, lo, hi), self.gcs.c(ch, lo, hi), self.pacc(bv[ch][pi], 0, hi - lo), ALU.mult)
                self.ts(self.u.c(ch, 0, H0), self.u.c(ch, 0, H0), hmcol, ALU.mult)
                w0 = self.colcol(CV_CW + 0 * CC + cg_)
                w1 = self.colcol(CV_CW + 1 * CC + cg_)
                w2 = self.colcol(CV_CW + 2 * CC + cg_)
                self.ts(self.yc.c(ch), self.u.c(ch, H0, TW), w2, ALU.mult)
                self.stt(self.yc.c(ch), self.u.c(ch, H0 - 1, TW - 1), w1, self.yc.c(ch), ALU.mult, ALU.add)
                self.stt(self.yc.c(ch), self.u.c(ch, H0 - 2, TW - 2), w0, self.yc.c(ch), ALU.mult, ALU.add)
            bb = self.proj(self.w_in, 0, KC, 0 + cp * 256, 2, self.H, own, kp=KC)
            for ch in range(2):
                self.tt(self.Z.c(cp * 2 + ch), self.yc.c(ch), self.pacc(bb[ch][0]), ALU.mult)

        for cp in range(CC // 2):
            bp = self.proj(self.w_in, 0, KC, 3 * c.CW + cp * 256, 2, self.H, halves, kp=KC)
            for ch in range(2):
                pc = cp * 2 + ch
                g = pc // c.CPG
                kin = pc % c.CPG
                w = POOL_WINDOWS[g]
                p0 = self.pb
                for pi, (lo, hi) in enumerate(halves):
                    self.copy(p0.c(0, lo, hi), self.pacc(bp[ch][pi], 0, hi - lo), eng="act")
                self.ts(p0.c(0, 0, H0), p0.c(0, 0, H0), hmcol, ALU.mult)
                src = 0
                for i in range(g + 1):
                    sh = 1 << i
                    st = (1 << (i + 1)) - 1
                    dst = 1 + (i % 2)
                    self.tt(self.pb.c(dst, st, TW), self.pb.c(src, st, TW), self.pb.c(src, st - sh, TW - sh), ALU.add)
                    src = dst
                pl = self.pooled[g % 2]
                ic = self.icnt.c(0, (t * 4 + g) * 16, (t * 4 + g) * 16 + 16)
                self.tt(self.t16.c(0), self.pb.c(src, H0, H0 + 16), ic, ALU.mult)
                self.tt(pl.c(kin, 0, 16), self.t16.c(0), p0.c(0, H0, H0 + 16), ALU.subtract)
                self.stt(pl.c(kin, 16, TT), self.pb.c(src, H0 + 16, TW), 1.0 / w, p0.c(0, H0 + 16, TW),
                         ALU.mult, ALU.subtract)
                if kin == c.CPG - 1:
                    pan = self.load_panel(self.w_pg, g * c.CPG, c.CPG, 0, c.PGS)
                    for e_ in range(c.CPG):
                        b = self.bank()
                        for kc in range(c.CPG):
                            self.mm(self.pacc(b), pan.c(kc, e_ * 128, (e_ + 1) * 128), pl.c(kc), kc == 0, kc == c.CPG - 1)
                        mc = g * c.CPG + e_
                        self.act(self.MX.c(mc), self.pacc(b), AF.Copy, scale=self.colcol(CV_PS + mc))

        for cg in range(c.D // 512):
            bgc = self.proj(self.w_in, 0, KC, 4 * c.CW + cg * 512, 4, self.H, own)
            for j in range(4):
                self.act(self.sgA.c(j), self.pacc(bgc[j][0]), AF.Sigmoid, bias=self.colcol(CV_BGC + cg * 4 + j))
            byc = self.proj(self.w_conv_out, 0, CC, cg * 512, 4, self.Z, [(0, TT)])
            for j in range(4):
                self.tt(self.sgA.c(j), self.sgA.c(j), self.pacc(byc[j][0]), ALU.mult)
            bgp = self.proj(self.w_in, 0, KC, 4 * c.CW + c.D + cg * 512, 4, self.H, own)
            for j in range(4):
                self.act(self.sgB.c(j), self.pacc(bgp[j][0]), AF.Sigmoid, bias=self.colcol(CV_BGP + cg * 4 + j))
            byp = self.proj(self.w_pool_out, 0, CC, cg * 512, 4, self.MX, [(0, TT)])
            for j in range(4):
                self.tt(self.sgB.c(j), self.sgB.c(j), self.pacc(byp[j][0]), ALU.mult)
                self.tt(self.MG.c(cg * 4 + j), self.sgA.c(j), self.sgB.c(j), ALU.add)

        if t == 0:
            self.mod_rest()
        self.load_x_T(t, self.stgH, False)
        for cg in range(c.D // 512):
            bo = self.proj(self.w_o, 0, KC, cg * 512, 4, self.MG, [(0, TT)])
            for j in range(4):
                n = cg * 4 + j
                self.stt(self.T.c(n, H0, TW), self.pacc(bo[j][0]), self.modc.c(0, G1 + n, G1 + n + 1),
                         self.T.c(n, H0, TW), ALU.mult, ALU.add)

        self.norm_to_h(own, KC, SH2)

        nfg = (c.FC + 3) // 4
        for fg in range(nfg):
            ncf = min(4, c.FC - 4 * fg)
            bgt = self.proj(self.w_gu, 0, KC, fg * 512, ncf, self.H, own)
            for j in range(ncf):
                self.act(self.sgt.c(j), self.pacc(bgt[j][0]), AF.Silu)
            bup = self.proj(self.w_gu, 0, KC, c.F + fg * 512, ncf, self.H, own)
            for j in range(ncf):
                self.tt(self.actb.c(j), self.sgt.c(j), self.pacc(bup[j][0]), ALU.mult)
            for hc in range(c.D // c.DW):
                pan = self.load_panel(self.w_down, fg * 4, ncf, hc * c.DW, c.DW)
                for j in range(c.DW // 128):
                    n = hc * (c.DW // 128) + j
                    b = self.bank()
                    for fc in range(ncf):
                        self.mm(self.pacc(b), pan.c(fc, j * 128, (j + 1) * 128), self.actb.c(fc), fc == 0, fc == ncf - 1)
                    self.stt(self.T.c(n, H0, TW), self.pacc(b), self.modc.c(0, G2 + n, G2 + n + 1),
                             self.T.c(n, H0, TW), ALU.mult, ALU.add)

        self.rms_rstd(own)
        qn = c.XW // 128
        idn = self.ident.c(0)
        for jb in range(TT // 128):
            w0 = H0 + jb * 128
            for ch in range(KC):
                self.stt(self.T.c(ch, w0, w0 + 128), self.T.c(ch, w0, w0 + 128), self.colcol(CV_GF + ch),
                         self.rstd.c(0, w0, w0 + 128), ALU.mult, ALU.mult)
            for hh in range(c.D // c.XW):
                sv = self.stgH[self.stg_i % 2]
                si = self.stg_i % 2
                self.stg_i += 1
                for q4 in range(qn // 4):
                    b = self.bank()
                    for qq in range(4):
                        ch = hh * qn + q4 * 4 + qq
                        o = self.pacc(b, qq * 128, (qq + 1) * 128)
                        i = self.T.c(ch, w0, w0 + 128)
                        S.add("pe", lambda e, o=o.ap, i=i.ap, d=idn.ap: e.transpose(o, i, d), reads=[i, idn], writes=[o])
                    self.copy(sv.c(0, q4 * 512, (q4 + 1) * 512), self.pacc(b), eng="act")
                s_ = sv.c(0)
                r0 = t * TT + jb * 128
                dst = self.y[r0:r0 + 128, hh * c.XW:(hh + 1) * c.XW]
                S.add("sp", lambda e, o=dst, i=s_.ap: e.dma_start(out=o, in_=i), reads=[s_],
                      writes=[[("y", t, jb, hh)]], dma="st%d" % si)

    def finish(self):
        c = self.cfg
        allw = [[("y", t, jb, hh)] for t in range(c.NT) for jb in range(c.TT // 128) for hh in range(c.D // c.XW)]
        self.S.add("sp", lambda e: e.nop(), reads=allw)

    def build(self):
        self.declare()
        self.alloc()
        self.setup()
        for t in range(self.cfg.NT):
            self.tile(t)
        self.finish()
        nc = self.nc
        S = self.S
        dkeys = S.finalize()
        with contextlib.ExitStack() as es:
            sems = {}
            for k in list(ENGS) + dkeys:
                sems[k] = es.enter_context(nc.semaphore("s_" + k))
            block = es.enter_context(nc.Block())

            @block.tensor
            def _(e):
                S.emit("pe", e, sems)

            @block.scalar
            def _(e):
                S.emit("act", e, sems)

            @block.vector
            def _(e):
                S.emit("dve", e, sems)

            @block.gpsimd
            def _(e):
                S.emit("pool", e, sems)

            @block.sync
            def _(e):
                S.emit("sp", e, sems)
        return nc


def host_inputs(cfg, specs, x, c, w_ada, b_ada, norm1_gain, w_in, b_branch_gate, conv_w, w_conv_out, w_pool_group,
                pool_scale, w_pool_out, w_o, norm2_gain, w_gate_up, w_down, final_norm_gain):
    f = lambda a: np.ascontiguousarray(np.asarray(a, dtype=np.float32))
    D = cfg.D
    x2 = f(x).reshape(cfg.S, D)
    rows = [f(c).reshape(-1), f(b_ada).reshape(-1), f(norm1_gain).reshape(-1), f(norm2_gain).reshape(-1),
            f(final_norm_gain).reshape(-1), f(b_branch_gate).reshape(-1), f(conv_w).reshape(-1), f(pool_scale).reshape(-1)]
    vec = np.concatenate(rows)
    assert vec.size == cfg.NV * 128
    vecs = np.zeros((cfg.NVB * 128, 128), np.float32)
    vecs.reshape(-1)[:vec.size] = vec
    ident = np.eye(128, dtype=np.float32)
    mats = {
        "w_ada": f(w_ada).reshape(D, 6 * D), "w_in": f(w_in).reshape(D, 4 * D),
        "w_conv_out": f(w_conv_out).reshape(cfg.CW, D), "w_pool_group": f(w_pool_group).reshape(4 * cfg.PGS, cfg.PGS),
        "w_pool_out": f(w_pool_out).reshape(cfg.CW, D), "w_o": f(w_o).reshape(D, D),
        "w_gate_up": f(w_gate_up).reshape(D, 2 * cfg.F), "w_down": f(w_down).reshape(cfg.F, D),
    }
    shared = {"vecs": vecs, "ident": ident}
    for name, order in specs.items():
        W = mats[name]
        flat = np.empty(W.size, np.float32)
        o = 0
        for (r0, nk, c0, ncols) in order:
            blk = W[r0 * 128:(r0 + nk) * 128, c0:c0 + ncols].reshape(nk, 128, ncols)
            n = 128 * nk * ncols
            flat[o:o + n].reshape(128, nk, ncols)[...] = blk.transpose(1, 0, 2)
            o += n
        assert o == W.size, (name, o, W.size)
        shared[name] = flat
    xpad = np.concatenate([np.zeros((cfg.HALO, D), np.float32), x2], axis=0)
    in_maps = []
    for core in range(cfg.NCORES):
        m = dict(shared)
        m["xs"] = np.ascontiguousarray(xpad[core * cfg.TPC: (core + 1) * cfg.TPC + cfg.HALO])
        hm = np.ones((128, cfg.NT), np.float32)
        icnt = np.zeros((cfg.NT, 4, 16), np.float32)
        for t in range(cfg.NT):
            g0 = core * cfg.TPC + t * cfg.TT
            if g0 == 0:
                hm[:, t] = 0.0
            for g, w in enumerate(POOL_WINDOWS):
                pos = g0 + np.arange(16)
                icnt[t, g] = 1.0 / np.minimum(pos + 1, w)
        m["hm"] = hm
        m["icnt"] = np.ascontiguousarray(np.broadcast_to(icnt.reshape(1, -1), (128, cfg.NT * 64)))
        in_maps.append(m)
    return in_maps


_NC_CACHE = {}


def run(cfg, inputs):
    key = (cfg.D, cfg.F, cfg.S, cfg.KP, cfg.NS)
    if key not in _NC_CACHE:
        b = Builder(cfg)
        nc = b.build()
        _NC_CACHE[key] = (nc, {w.name: list(w.order) for w in b.wrefs})
    nc, specs = _NC_CACHE[key]
    in_maps = host_inputs(cfg, specs, **inputs)
    res = run_bass_kernel_spmd(nc, in_maps, core_ids=list(range(cfg.NCORES)))
    out = np.concatenate([np.asarray(r["y"]) for r in res.results], axis=0)
    return out.reshape(1, cfg.S, cfg.D).astype(np.float32)


def kernel(**inputs):
    cfg = Cfg()
    return run(cfg, inputs)
```
